# Optimizing a Trainium2 kernel written in Bass

```python
import math
import jax, jax.numpy as jnp
from jax import lax
import numpy as np

D_MODEL = 1024
BATCH = 8
SEQ = 4096
DEPTH = 4

N_HEADS = 8
HEAD_DIM = D_MODEL // (2 * N_HEADS)
V_DIM = 2 * HEAD_DIM
ATTN_WIDTH = N_HEADS * V_DIM
CONV_CH = D_MODEL
CONV_WIDTH = 31
FF_DIM = ((8 * D_MODEL // 3 + 255) // 256) * 256
NUM_BUCKETS = 32
MAX_DISTANCE = 128
Q_BLOCK = 128
EPS = 1e-6

Q_COLS = N_HEADS * 2 * HEAD_DIM
K_COLS = N_HEADS * 2 * HEAD_DIM
V_COLS = N_HEADS * V_DIM
U_COLS = 2 * CONV_CH
G_COLS = 2 * D_MODEL
IN_COLS = Q_COLS + K_COLS + V_COLS + U_COLS + G_COLS

kernel_name = "hybrid_diffattn_conformer_gated"


def rms_norm(x, w):
    xf = x.astype(jnp.float32)
    y = xf * lax.rsqrt(jnp.mean(xf * xf, axis=-1, keepdims=True) + EPS)
    return (y * w.astype(jnp.float32)).astype(x.dtype)


def layer_norm(x, w, b):
    xf = x.astype(jnp.float32)
    mu = jnp.mean(xf, axis=-1, keepdims=True)
    xc = xf - mu
    var = jnp.mean(xc * xc, axis=-1, keepdims=True)
    y = xc * lax.rsqrt(var + EPS) * w.astype(jnp.float32) + b.astype(jnp.float32)
    return y.astype(x.dtype)


def t5_bucket(rel):
    n = jnp.maximum(rel, 0)
    max_exact = NUM_BUCKETS // 2
    nf = jnp.maximum(n, 1).astype(jnp.float32)
    large = max_exact + (jnp.log(nf / max_exact) / math.log(MAX_DISTANCE / max_exact)
                         * (NUM_BUCKETS - max_exact)).astype(jnp.int32)
    large = jnp.minimum(large, NUM_BUCKETS - 1)
    return jnp.where(n < max_exact, n, large)


def diff_attention(q1, q2, k1, k2, v, lam, rel_bias):
    B, H, S, _ = q1.shape
    nb = S // Q_BLOCK
    scale = HEAD_DIM ** -0.5
    f32 = jnp.float32
    k1f, k2f, vf = k1.astype(f32), k2.astype(f32), v.astype(f32)
    lamf = lam.astype(f32)
    k_pos = jnp.arange(S, dtype=jnp.int32)
    bias_tab = rel_bias.astype(f32)

    def block(i):
        start = i * Q_BLOCK
        qb1 = lax.dynamic_slice_in_dim(q1, start, Q_BLOCK, axis=2).astype(f32)
        qb2 = lax.dynamic_slice_in_dim(q2, start, Q_BLOCK, axis=2).astype(f32)
        q_pos = start + jnp.arange(Q_BLOCK, dtype=jnp.int32)
        rel = q_pos[:, None] - k_pos[None, :]
        bias = jnp.transpose(bias_tab[t5_bucket(rel)], (2, 0, 1))
        causal = rel >= 0

        def probs(qb, kf):
            s = jnp.einsum('bhqd,bhkd->bhqk', qb, kf) * scale + bias
            s = jnp.where(causal, s, -jnp.inf)
            return jax.nn.softmax(s, axis=-1)

        p = probs(qb1, k1f) - lamf * probs(qb2, k2f)
        return jnp.einsum('bhqk,bhkd->bhqd', p, vf)

    out = lax.map(block, jnp.arange(nb))
    out = jnp.transpose(out, (1, 0, 3, 2, 4)).reshape(B, S, H, V_DIM)
    return out.astype(q1.dtype)


def causal_depthwise_conv(u, w, b):
    up = jnp.pad(u, ((0, 0), (CONV_WIDTH - 1, 0), (0, 0)))
    y = lax.conv_general_dilated(up, w[:, None, :], window_strides=(1,), padding='VALID',
                                 dimension_numbers=('NWC', 'WIO', 'NWC'),
                                 feature_group_count=CONV_CH)
    return y + b


def setup_inputs(seed: int = 0) -> dict:
    key = jax.random.key(seed)
    ks = jax.random.split(key, 24)
    n = jax.random.normal
    f = jnp.float32

    def gain(k, shape):
        return 1.0 + 0.05 * n(k, shape, f)

    return {
        "x": n(ks[0], (BATCH, SEQ, D_MODEL), f),
        "attn_norm_w": gain(ks[1], (DEPTH, D_MODEL)),
        "w_in": n(ks[2], (DEPTH, D_MODEL, IN_COLS), f) * D_MODEL ** -0.5,
        "q_norm_w": gain(ks[3], (DEPTH, HEAD_DIM)),
        "k_norm_w": gain(ks[4], (DEPTH, HEAD_DIM)),
        "lam_q1": 0.1 * n(ks[5], (DEPTH, HEAD_DIM), f),
        "lam_k1": 0.1 * n(ks[6], (DEPTH, HEAD_DIM), f),
        "lam_q2": 0.1 * n(ks[7], (DEPTH, HEAD_DIM), f),
        "lam_k2": 0.1 * n(ks[8], (DEPTH, HEAD_DIM), f),
        "subln_w": gain(ks[9], (DEPTH, V_DIM)),
        "w_proj_attn": n(ks[10], (DEPTH, ATTN_WIDTH, D_MODEL), f) * ATTN_WIDTH ** -0.5,
        "conv_w": n(ks[11], (DEPTH, CONV_WIDTH, CONV_CH), f) * CONV_WIDTH ** -0.5,
        "conv_b": 0.02 * n(ks[12], (DEPTH, CONV_CH), f),
        "conv_ln_w": gain(ks[13], (DEPTH, CONV_CH)),
        "conv_ln_b": 0.02 * n(ks[14], (DEPTH, CONV_CH), f),
        "w_proj_conv": n(ks[15], (DEPTH, CONV_CH, D_MODEL), f) * CONV_CH ** -0.5,
        "gate_b": 0.02 * n(ks[16], (DEPTH, G_COLS), f),
        "w_out": n(ks[17], (DEPTH, D_MODEL, D_MODEL), f) * D_MODEL ** -0.5,
        "ffn_norm_w": gain(ks[18], (DEPTH, D_MODEL)),
        "w_gate_up": n(ks[19], (DEPTH, D_MODEL, 2 * FF_DIM), f) * D_MODEL ** -0.5,
        "w_down": n(ks[20], (DEPTH, FF_DIM, D_MODEL), f) * FF_DIM ** -0.5,
        "rel_bias": 0.5 * n(ks[21], (NUM_BUCKETS, N_HEADS), f),
    }


def reference(x, attn_norm_w, w_in, q_norm_w, k_norm_w, lam_q1, lam_k1, lam_q2, lam_k2,
              subln_w, w_proj_attn, conv_w, conv_b, conv_ln_w, conv_ln_b, w_proj_conv,
              gate_b, w_out, ffn_norm_w, w_gate_up, w_down, rel_bias):
    B, S, _ = x.shape
    splits = np.cumsum([Q_COLS, K_COLS, V_COLS, U_COLS]).tolist()
    for l in range(DEPTH):
        lam_init = 0.8 - 0.6 * math.exp(-0.3 * l)
        h = rms_norm(x, attn_norm_w[l])
        proj = h @ w_in[l]
        q, k, v, u, g = jnp.split(proj, splits, axis=-1)

        q = rms_norm(q.reshape(B, S, N_HEADS, 2, HEAD_DIM), q_norm_w[l])
        k = rms_norm(k.reshape(B, S, N_HEADS, 2, HEAD_DIM), k_norm_w[l])
        q = jnp.transpose(q, (3, 0, 2, 1, 4))
        k = jnp.transpose(k, (3, 0, 2, 1, 4))
        v = jnp.transpose(v.reshape(B, S, N_HEADS, V_DIM), (0, 2, 1, 3))
        lam = (jnp.exp(jnp.sum(lam_q1[l].astype(jnp.float32) * lam_k1[l].astype(jnp.float32)))
               - jnp.exp(jnp.sum(lam_q2[l].astype(jnp.float32) * lam_k2[l].astype(jnp.float32)))
               + lam_init)
        a = diff_attention(q[0], q[1], k[0], k[1], v, lam, rel_bias)
        a = rms_norm(a, subln_w[l]) * (1.0 - lam_init)
        y_a = a.reshape(B, S, ATTN_WIDTH) @ w_proj_attn[l]

        u_a, u_b = jnp.split(u, 2, axis=-1)
        c = causal_depthwise_conv(u_a * jax.nn.sigmoid(u_b), conv_w[l], conv_b[l])
        c = jax.nn.silu(layer_norm(c, conv_ln_w[l], conv_ln_b[l]))
        y_b = c @ w_proj_conv[l]

        g_a, g_b = jnp.split(jax.nn.sigmoid(g + gate_b[l]), 2, axis=-1)
        x = x + (g_a * y_a + g_b * y_b) @ w_out[l]

        h = rms_norm(x, ffn_norm_w[l])
        gt, up = jnp.split(h @ w_gate_up[l], 2, axis=-1)
        x = x + (jax.nn.silu(gt) * up) @ w_down[l]
    return x
```

```python
import math
import numpy as np
from contextlib import ExitStack
import concourse.bass as bass
import concourse.mybir as mybir
from concourse.bass_utils import run_bass_kernel_spmd

F32 = mybir.dt.float32
BF16 = mybir.dt.bfloat16
AF = mybir.ActivationFunctionType
ALU = mybir.AluOpType
AX = mybir.AxisListType

D = 1024
NH = 8
FF = 2816
NF = FF // 128
CW = 31
INC = 7168
EPS = 1e-6
NEG = -30000.0
GL_L = 1152


class DSem:
    def __init__(self, sem):
        self.sem = sem
        self.count = 0


class Buf:
    __slots__ = ("name", "writers", "readers", "dsem", "prev_readers", "prev_writers")

    def __init__(self, name, dsem=None):
        self.name = name
        self.writers = {}
        self.readers = {}
        self.prev_readers = {}
        self.prev_writers = {}
        self.dsem = dsem


class Eng:
    def __init__(self, name):
        self.name = name
        self.sem = None
        self.count = 0
        self.ops = []
        self.waited = {}
        self.pending = []


class Sched:
    def __init__(self, nc, stack):
        self.nc = nc
        self.stack = stack
        self.pe = Eng("tensor")
        self.act = Eng("scalar")
        self.dve = Eng("vector")
        self.pool = Eng("gpsimd")
        self.sp = Eng("sync")
        self.engs = [self.pe, self.act, self.dve, self.pool, self.sp]
        for e in self.engs:
            e.sem = stack.enter_context(nc.semaphore("es_" + e.name))
        self.dsems = []
        self.dcache = {}

    def dsem(self, name):
        if name in self.dcache:
            return self.dcache[name]
        d = DSem(self.stack.enter_context(self.nc.semaphore("ds_" + name)))
        self.dsems.append(d)
        self.dcache[name] = d
        return d

    def buf(self, name, dma=False):
        return Buf(name, self.dsem(name) if dma else None)

    def _resolve(self, tok, eng):
        if tok[0] == "e":
            if tok[1] is self.pe and eng is self.pe:
                return None
            assert tok[2] is not None, "unresolved token (mark producer tracked)"
            return (tok[1].sem, tok[2])
        d = tok[1]
        return (d.sem, 16 * d.count)

    def _waits(self, eng, reads, writes, partial):
        waits = {}
        toks = []
        for b in reads:
            toks.extend(b.writers.values())
        for b in writes:
            toks.extend(b.readers.values())
            if not partial:
                toks.extend(b.writers.values())
            else:
                toks.extend(b.prev_readers.values())
                toks.extend(b.prev_writers.values())
        for t in toks:
            r = self._resolve(t, eng)
            if r is None:
                continue
            sem, val = r
            k = id(sem)
            if eng.waited.get(k, 0) < val:
                eng.waited[k] = val
                waits[k] = (sem, val)
        return list(waits.values())

    def _register(self, tok, reads, writes, partial):
        key = (tok[0], id(tok[1]))
        for b in reads:
            b.readers[key] = tok
        for b in writes:
            if partial:
                b.writers[key] = tok
            else:
                b.prev_readers = b.readers
                b.prev_writers = b.writers
                b.writers = {key: tok}
                b.readers = {}

    def op(self, eng, fn, reads=(), writes=(), track=True, partial=False):
        r = Rec()
        fn(r)
        name, a, k = r.call
        fn = lambda e: getattr(e, name)(*a, **k)
        waits = self._waits(eng, reads, writes, partial)
        if track:
            eng.count += 1
            tok = ["e", eng, eng.count]
            for p in eng.pending:
                p[2] = eng.count
            eng.pending = []
            eng.ops.append((waits, fn, (eng.sem, 1)))
        else:
            tok = ["e", eng, None]
            eng.pending.append(tok)
            eng.ops.append((waits, fn, None))
        self._register(tok, reads, writes, partial)

    def dma(self, eng, out, in_, dbuf, reads=(), writes=(), partial=False):
        waits = self._waits(eng, reads, writes, partial)
        d = dbuf.dsem
        d.count += 1
        tok = ["d", d, d.count]
        eng.ops.append((waits, lambda e: e.dma_start(out=out, in_=in_), (d.sem, 16)))
        self._register(tok, reads, writes, partial)

    def barrier(self):
        targets = []
        for d in self.dsems:
            if d.count:
                targets.append((d.sem, 16 * d.count))
        for e in self.engs:
            if e.count:
                assert not e.pending, "pending untracked ops at barrier on " + e.name
                targets.append((e.sem, e.count))
        for e in self.engs:
            waits = []
            for sem, val in targets:
                if sem is e.sem and e is not self.pe and False:
                    continue
                k = id(sem)
                if e.waited.get(k, 0) < val:
                    e.waited[k] = val
                    waits.append((sem, val))
            e.ops.append((waits, None, None))

    def emit(self):
        nc = self.nc
        block = self.stack.enter_context(nc.Block())

        def run(eng):
            def f(e):
                for waits, fn, inc in eng.ops:
                    for sem, val in waits:
                        e.wait_ge(sem, val)
                    if fn is not None:
                        ins = fn(e)
                        if inc is not None:
                            ins.then_inc(inc[0], inc[1])
            return f

        block.tensor(run(self.pe))
        block.scalar(run(self.act))
        block.vector(run(self.dve))
        block.gpsimd(run(self.pool))
        block.sync(run(self.sp))


class Rec:
    def __init__(self):
        self.call = None

    def __getattr__(self, name):
        def f(*a, **k):
            self.call = (name, a, k)
            return self
        return f


class Rot:
    def __init__(self, items):
        self.items = items
        self.i = 0

    def get(self):
        it = self.items[self.i % len(self.items)]
        self.i += 1
        return it


def lam_init(l):
    return 0.8 - 0.6 * math.exp(-0.3 * l)


def t5_bucket_np(n):
    n = np.maximum(n, 0)
    nf = np.maximum(n, 1).astype(np.float32)
    large = 16 + (np.log(nf / np.float32(16)) / np.float32(math.log(128 / 16)) * np.float32(16)).astype(np.int32)
    large = np.minimum(large, 31)
    return np.where(n < 16, n, large)


def onehot_table():
    oh = np.zeros((33, GL_L), np.float32)
    m = np.arange(GL_L)
    b = t5_bucket_np((m - 511).astype(np.int32))
    for i in range(GL_L):
        if i < 511:
            oh[32, i] = 1.0
        else:
            oh[b[i], i] = 1.0
    return oh


def build(S=4096, DEPTH=4, dbg=False):
    NT = S // 512
    NS = S // 256
    NB = S // 128
    FT = min(1024, S)
    NP = S // FT
    TCS = FT // 512
    nc = bass.Bass("TRN2", target_bir_lowering=False)
    okind = "ExternalOutput" if dbg else "Internal"

    def din(name, shape):
        return nc.dram_tensor(name, list(shape), F32, kind="ExternalInput").ap()

    xT = din("xT", [D, S])
    attn_norm_w = din("attn_norm_w", [DEPTH, D])
    w_in = din("w_in", [DEPTH, D, INC])
    q_norm_w = din("q_norm_w", [DEPTH, 64])
    k_norm_w = din("k_norm_w", [DEPTH, 64])
    lam_in = [din(n, [DEPTH, 64]) for n in ("lam_q1", "lam_k1", "lam_q2", "lam_k2")]
    subln_w = din("subln_w", [DEPTH, 128])
    w_proj_attn = din("w_proj_attn", [DEPTH, D, D])
    conv_w = din("conv_w", [DEPTH, CW, D])
    conv_b = din("conv_b", [DEPTH, D])
    conv_ln_w = din("conv_ln_w", [DEPTH, D])
    conv_ln_b = din("conv_ln_b", [DEPTH, D])
    w_proj_conv = din("w_proj_conv", [DEPTH, D, D])
    gate_b = din("gate_b", [DEPTH, 2 * D])
    w_out = din("w_out", [DEPTH, D, D])
    ffn_norm_w = din("ffn_norm_w", [DEPTH, D])
    w_gate_up = din("w_gate_up", [DEPTH, D, 2 * FF])
    w_down = din("w_down", [DEPTH, FF, D])
    rel_bias = din("rel_bias", [32, NH])
    oh_d = din("oh", [33, GL_L])
    outT = nc.dram_tensor("outT", [D, S], F32, kind="ExternalOutput").ap()

    QTd = nc.dram_tensor("QTd", [NH, 128, S], BF16, kind=okind).ap()
    KTd = nc.dram_tensor("KTd", [NH, 128, S], BF16, kind=okind).ap()
    VHd = nc.dram_tensor("VHd", [NH, 128, NB, 128], BF16, kind=okind).ap()
    Gd = nc.dram_tensor("Gd", [16, 128, S], BF16, kind=okind).ap()
    CTd = nc.dram_tensor("CTd", [8, 128, S], F32, kind=okind).ap()
    X1d = nc.dram_tensor("X1d", [D, S], F32, kind=okind).ap()
    X2d = nc.dram_tensor("X2d", [D, S], F32, kind=okind).ap()
    GRd = nc.dram_tensor("GRd", [NH, 128 * GL_L], F32, kind=okind).ap()
    ATd = nc.dram_tensor("ATd", [8, 128, S], BF16, kind=okind).ap() if dbg else None

    with ExitStack() as st:
        S_ = Sched(nc, st)
        pe, act, dve, pool, sp = S_.pe, S_.act, S_.dve, S_.pool, S_.sp

        uid = [0]

        def T(stack, name, shape, dt, dma=False):
            uid[0] += 1
            t = stack.enter_context(nc.sbuf_tensor(f"{name}_u{uid[0]}", list(shape), dt))
            return t, S_.buf(name, dma)

        def TR(stack, name, shape, dt, n, dma=False):
            return Rot([T(stack, f"{name}{i}", shape, dt, dma) for i in range(n)])

        actT, _ = T(st, "actT", [128, 8, S], BF16)
        ACTB = [[S_.buf(f"act{c}_{s}") for s in range(NS)] for c in range(8)]
        ones_bf, ones_bfb = T(st, "ones_bf", [128, 128], BF16)
        blk_bf, blk_bfb = T(st, "blk_bf", [128, 128], BF16)
        ident_f, ident_fb = T(st, "ident_f", [128, 128], F32)
        ident_bf, ident_bfb = T(st, "ident_bf", [128, 128], BF16)
        PRA, PRAb = T(st, "PRA", [128, 8, 28], F32)
        PRB, PRBb = T(st, "PRB", [128, 8, 4 * CW], F32)
        PRC, PRCb = T(st, "PRC", [128, 12], F32)
        QS, QSb = T(st, "QS", [128, 4], F32)
        SL, SLb = T(st, "SL", [128, 4], F32)
        NLAM, NLAMb = T(st, "NLAM", [128, 4], F32)
        B31, B31b = T(st, "B31", [128, NH], F32)
        PSALL = st.enter_context(nc.psum_tensor("psall", [128, 4096], F32))
        PS = [(PSALL[:, i * 512:(i + 1) * 512], S_.buf(f"ps{i}")) for i in range(8)]
        PSPAIR = [(PSALL[:, i * 1024:(i + 1) * 1024], S_.buf(f"pspair{i}")) for i in range(2)]

        def mm(out, lhsT, rhs, start, stop, reads, writes, track):
            S_.op(pe, lambda e: e.matmul(out, lhsT, rhs, start=start, stop=stop),
                  reads=reads, writes=writes, track=track, partial=not start)

        with ExitStack() as ss:
            ones_f, ones_fb = T(ss, "ones_f", [128, 128], F32)
            PA, PAb = T(ss, "PA", [28, 1024], F32, True)
            PB, PBb = T(ss, "PB", [4 * CW, 1024], F32, True)
            PC, PCb = T(ss, "PC", [12, 128], F32, True)
            LV = [T(ss, f"LV{i}", [128, 4 * 64], F32, True) for i in range(4)]
            TAB, TABb = T(ss, "TAB", [33, NH], F32, True)
            OHs, OHb = T(ss, "OHs", [33, GL_L], F32, True)
            LT, LTb = T(ss, "LT", [33, 128], F32)
            GR = TR(ss, "GR", [128, GL_L], F32, 2, True)
            lt1, lt1b = T(ss, "lt1", [128, 256], F32)
            lt2, lt2b = T(ss, "lt2", [128, 8], F32)

            S_.op(pool, lambda e: e.memset(ones_f[:], 1.0), writes=[ones_fb])
            S_.op(pool, lambda e: e.memset(ones_bf[:], 1.0), writes=[ones_bfb])
            S_.op(pool, lambda e: e.memset(ident_f[:], 1.0), writes=[ident_fb])
            S_.op(pool, lambda e: e.affine_select(ident_f[:], ident_f[:], [[-1, 128]], ALU.is_equal, 0.0,
                                                  base=0, channel_multiplier=1),
                  reads=[ident_fb], writes=[ident_fb])
            S_.op(dve, lambda e: e.tensor_copy(ident_bf[:], ident_f[:]), reads=[ident_fb], writes=[ident_bfb])
            S_.op(pool, lambda e: e.memset(blk_bf[:], 0.0), writes=[blk_bfb])
            S_.op(pool, lambda e: e.memset(blk_bf[0:64, 0:64], 1.0), reads=[blk_bfb], writes=[blk_bfb])
            S_.op(pool, lambda e: e.memset(blk_bf[64:128, 64:128], 1.0), reads=[blk_bfb], writes=[blk_bfb])

            for i, src in enumerate((attn_norm_w, ffn_norm_w, conv_b, conv_ln_w, conv_ln_b)):
                S_.dma(sp, PA[4 * i:4 * i + DEPTH, :], src, PAb, writes=[PAb], partial=True)
            S_.dma(sp, PA[20:20 + 2 * DEPTH, :], gate_b.rearrange("l (h n) -> (l h) n", h=2), PAb, writes=[PAb],
                   partial=True)
            S_.dma(sp, PB[0:DEPTH * CW, :], conv_w.rearrange("l j n -> (l j) n"), PBb, writes=[PBb])
            S_.dma(sp, PC[0:DEPTH, 0:64], q_norm_w, PCb, writes=[PCb], partial=True)
            S_.dma(sp, PC[0:DEPTH, 64:128], q_norm_w, PCb, writes=[PCb], partial=True)
            S_.dma(sp, PC[4:4 + DEPTH, 0:64], k_norm_w, PCb, writes=[PCb], partial=True)
            S_.dma(sp, PC[4:4 + DEPTH, 64:128], k_norm_w, PCb, writes=[PCb], partial=True)
            S_.dma(sp, PC[8:8 + DEPTH, :], subln_w, PCb, writes=[PCb], partial=True)
            for i in range(4):
                S_.dma(sp, LV[i][0][:, 0:DEPTH * 64],
                       lam_in[i].rearrange("l d -> (l d)").rearrange("(o n) -> o n", o=1).partition_broadcast(128),
                       LV[i][1], writes=[LV[i][1]])
            S_.dma(sp, TAB[0:32, :], rel_bias, TABb, writes=[TABb], partial=True)
            S_.op(pool, lambda e: e.memset(TAB[32:33, :], NEG), writes=[TABb], partial=True)
            S_.dma(sp, OHs[:], oh_d, OHb, writes=[OHb])

            nrA = 20 + 2 * DEPTH
            p0, p0b = PS[0]
            for cc in range(8):
                S_.op(pe, lambda e, cc=cc: e.transpose(p0[:, cc * 28:cc * 28 + nrA], PA[0:nrA, cc * 128:(cc + 1) * 128],
                                                       ident_f[0:nrA, 0:nrA]),
                      reads=[PAb, ident_fb], writes=[p0b], track=(cc == 7), partial=True)
            S_.op(dve, lambda e: e.tensor_copy(PRA[:].rearrange("p c r -> p (c r)"), p0[:, 0:224]),
                  reads=[p0b], writes=[PRAb])
            nrB = DEPTH * CW
            for half in range(2):
                ph, phb = PS[1 + half]
                for k in range(4):
                    cc = half * 4 + k
                    S_.op(pe, lambda e, cc=cc, k=k, ph=ph: e.transpose(
                        ph[:, k * 124:k * 124 + nrB], PB[0:nrB, cc * 128:(cc + 1) * 128], ident_f[0:nrB, 0:nrB]),
                        reads=[PBb, ident_fb], writes=[phb], track=(k == 3), partial=True)
                S_.op(dve, lambda e, half=half, ph=ph: e.tensor_copy(
                    PRB[:, half * 4:half * 4 + 4, :].rearrange("p c r -> p (c r)"), ph[:, 0:496]),
                    reads=[phb], writes=[PRBb], partial=True)
            p3, p3b = PS[3]
            nrC = 8 + DEPTH
            S_.op(pe, lambda e: e.transpose(p3[:, 0:nrC], PC[0:nrC, :], ident_f[0:nrC, 0:nrC]),
                  reads=[PCb, ident_fb], writes=[p3b])
            S_.op(dve, lambda e: e.tensor_copy(PRC[:, 0:nrC], p3[:, 0:nrC]), reads=[p3b], writes=[PRCb])
            S_.op(dve, lambda e: e.tensor_scalar(QS[:, 0:DEPTH], PRC[:, 0:DEPTH], 0.125, None, ALU.mult),
                  reads=[PRCb], writes=[QSb])
            for l in range(DEPTH):
                S_.op(dve, lambda e, l=l: e.tensor_scalar(SL[:, l:l + 1], PRC[:, 8 + l:9 + l], 1.0 - lam_init(l), None,
                                                          ALU.mult),
                      reads=[PRCb], writes=[SLb], partial=True)
            for k in range(2):
                a_t, a_b = LV[2 * k]
                b_t, b_b = LV[2 * k + 1]
                S_.op(dve, lambda e, a_t=a_t, b_t=b_t: e.tensor_tensor(lt1[:, 0:DEPTH * 64], a_t[:, 0:DEPTH * 64],
                                                                     b_t[:, 0:DEPTH * 64], ALU.mult),
                      reads=[a_b, b_b], writes=[lt1b])
                S_.op(dve, lambda e, k=k: e.tensor_reduce(lt2[:, 4 * k:4 * k + DEPTH],
                                                          lt1[:, 0:DEPTH * 64].rearrange("p (l d) -> p l d", d=64),
                                                          AX.X, ALU.add),
                      reads=[lt1b], writes=[lt2b], partial=True)
            S_.op(act, lambda e: e.activation(lt2[:, 0:8], lt2[:, 0:8], AF.Exp), reads=[lt2b], writes=[lt2b]) \
                if DEPTH == 4 else [S_.op(act, lambda e, k=k: e.activation(lt2[:, 4 * k:4 * k + DEPTH],
                                                                         lt2[:, 4 * k:4 * k + DEPTH], AF.Exp),
                                          reads=[lt2b], writes=[lt2b]) for k in range(2)]
            S_.op(dve, lambda e: e.tensor_tensor(NLAM[:, 0:DEPTH], lt2[:, 4:4 + DEPTH], lt2[:, 0:DEPTH], ALU.subtract),
                  reads=[lt2b], writes=[NLAMb])
            for l in range(DEPTH):
                S_.op(dve, lambda e, l=l: e.tensor_scalar(NLAM[:, l:l + 1], NLAM[:, l:l + 1], -lam_init(l), None,
                                                          ALU.add),
                      reads=[NLAMb], writes=[NLAMb])
            for h in range(NH):
                S_.op(dve, lambda e, h=h: e.tensor_scalar(LT[:, :], ones_f[0:33, :], TAB[0:33, h:h + 1], None, ALU.mult),
                      reads=[ones_fb, TABb], writes=[LTb])
                gr, grb = GR.get()
                for k, (c0, cn) in enumerate(((0, 512), (512, 512), (1024, GL_L - 1024))):
                    pk, pkb = PS[4 + k]
                    S_.op(pe, lambda e, pk=pk, c0=c0, cn=cn: e.matmul(pk[:, 0:cn], LT[:, :], OHs[:, c0:c0 + cn],
                                                                      start=True, stop=True),
                          reads=[LTb, OHb], writes=[pkb])
                    S_.op(act if k == 1 else dve,
                          (lambda e, pk=pk, c0=c0, cn=cn, gr=gr: e.activation(gr[:, c0:c0 + cn], pk[:, 0:cn], AF.Copy))
                          if k == 1 else
                          (lambda e, pk=pk, c0=c0, cn=cn, gr=gr: e.tensor_copy(gr[:, c0:c0 + cn], pk[:, 0:cn])),
                          reads=[pkb], writes=[grb], partial=True)
                S_.op(dve, lambda e, h=h, gr=gr: e.tensor_copy(B31[:, h:h + 1], gr[:, 1150:1151]),
                      reads=[grb], writes=[B31b], partial=True)
                S_.dma(pool, GRd[h].rearrange("(p f) -> p f", f=GL_L), gr[:], grb, reads=[grb])
            S_.barrier()

        class WLoader:
            def __init__(self, stack, name, C, ncols, nstg, nwb):
                self.C, self.ncols = C, ncols
                self.stg = TR(stack, name + "_s", [128, C, ncols], F32, nstg, True)
                self.wb = TR(stack, name + "_w", [128, C, ncols], BF16, nwb)

            def load(self, src, scale=None, ncols=None):
                n = self.ncols if ncols is None else ncols
                sg, sgb = self.stg.get()
                wb, wbb = self.wb.get()
                S_.dma(sp, sg[:, :, 0:n], src, sgb, writes=[sgb])
                C = self.C
                hc = C // 2
                if scale is None:
                    S_.op(dve, lambda e: e.tensor_copy(wb[:, 0:hc, 0:n], sg[:, 0:hc, 0:n]), reads=[sgb], writes=[wbb])
                    S_.op(act, lambda e: e.activation(wb[:, hc:C, 0:n], sg[:, hc:C, 0:n], AF.Copy),
                          reads=[sgb], writes=[wbb], partial=True)
                else:
                    sc, scb = scale
                    for c in range(C):
                        if c % 2 == 0:
                            S_.op(dve, lambda e, c=c: e.tensor_scalar(wb[:, c, 0:n], sg[:, c, 0:n], sc[:, c:c + 1], None,
                                                                      ALU.mult),
                                  reads=[sgb, scb], writes=[wbb], track=(c >= C - 2), partial=(c > 0))
                        else:
                            S_.op(act, lambda e, c=c: e.activation(wb[:, c, 0:n], sg[:, c, 0:n], AF.Identity,
                                                                   scale=sc[:, c:c + 1]),
                                  reads=[sgb, scb], writes=[wbb], track=(c >= C - 2), partial=True)
                return wb, wbb

        def pipelined(n, load):
            nxt = load(0)
            for i in range(n):
                cur = nxt
                if i + 1 < n:
                    nxt = load(i + 1)
                yield i, cur

        def rstd_from(ps_t, ps_b, n_inv, R, Rb, w=512):
            S_.op(act, lambda e: e.activation(R[:, 0:w], ps_t[:, 0:w], AF.Sqrt, bias=EPS, scale=n_inv),
                  reads=[ps_b], writes=[Rb])
            S_.op(dve, lambda e: e.reciprocal(R[:, 0:w], R[:, 0:w]), reads=[Rb], writes=[Rb])

        def actbufs(c, t):
            return [ACTB[c][2 * t], ACTB[c][2 * t + 1]]

        import os
        for l in range(DEPTH):
            if l >= int(os.environ.get('KSTOP', '99')):
                break
            xin = xT if l == 0 else X2d
            xout = outT if l == DEPTH - 1 else X2d
            xin_v = xin.rearrange("(c p) s -> p c s", p=128)
            win_v = w_in[l].rearrange("(c p) n -> p c n", p=128)

            with ExitStack() as ph:
                XC = TR(ph, "XC", [128, 8, 512], F32, 2, True)
                SQ = TR(ph, "SQ1", [128, 8, 512], BF16, 2)
                RR = TR(ph, "R1", [128, 512], F32, 2)
                for t in range(NT):
                    xc, xcb = XC.get()
                    sq, sqb = SQ.get()
                    R, Rb = RR.get()
                    S_.dma(sp, xc[:], xin_v[:, :, t * 512:(t + 1) * 512], xcb, writes=[xcb])
                    S_.op(act, lambda e, xc=xc, sq=sq: e.activation(sq[:], xc[:], AF.Square), reads=[xcb], writes=[sqb])
                    p, pb = PS[t % 2]
                    for c in range(8):
                        mm(p[:], ones_bf[:], sq[:, c, :], c == 0, c == 7, [sqb, ones_bfb], [pb], c == 7)
                    rstd_from(p, pb, 1.0 / D, R, Rb)
                    for c in range(8):
                        eng = dve
                        S_.op(eng, lambda e, c=c, xc=xc, R=R, t=t: e.tensor_tensor(
                            actT[:, c, t * 512:(t + 1) * 512], xc[:, c, :], R[:], ALU.mult),
                            reads=[xcb, Rb], writes=actbufs(c, t))
                S_.barrier()

            with ExitStack() as ph:
                WL = WLoader(ph, "wi", 8, 256, 2, 4)
                anw = (PRA[:, :, l], PRAb)
                SQb = TR(ph, "sq2", [128, 512], BF16, 3)
                R2 = TR(ph, "R2", [128, 512], F32, 3)
                ROW = TR(ph, "row", [128, 512], BF16, 4, True)
                VT = TR(ph, "vt", [128, 4, 512], BF16, 2, True)
                GLs = []
                for i in range(2):
                    g, _ = T(ph, f"GL{i}", [128, 32 + S], BF16)
                    gb_ = [S_.buf(f"GL{i}_pad")] + [S_.buf(f"GL{i}_{t}") for t in range(NT)]
                    S_.op(pool, lambda e, g=g: e.memset(g[:, 0:32], 0.0), writes=[gb_[0]])
                    GLs.append((g, gb_))
                DG = TR(ph, "dg", [128, CW, 128], BF16, 2)
                SG = TR(ph, "sg", [128, 512], F32, 2)
                CO = TR(ph, "co", [128, 512], F32, 3, True)

                for sec in range(2):
                    dst = QTd if sec == 0 else KTd
                    wv = (QS if sec == 0 else PRC)
                    wvb = (QSb if sec == 0 else PRCb)
                    wcol = (lambda l_: l_) if sec == 0 else (lambda l_: 4 + l_)
                    for g, (wb, wbb) in pipelined(4, lambda g_: WL.load(
                            win_v[:, :, sec * 1024 + g_ * 256: sec * 1024 + (g_ + 1) * 256], anw)):
                        items = [(i, t) for i in range(2) for t in range(NT)]
                        state = {}

                        def stA(k):
                            i, t = items[k]
                            p, pb = PS[k % 4]
                            for c in range(8):
                                mm(p[:], wb[:, c, i * 128:(i + 1) * 128], actT[:, c, t * 512:(t + 1) * 512],
                                   c == 0, c == 7, [wbb] + actbufs(c, t), [pb], c == 7)
                            sq, sqb = SQb.get()
                            S_.op(act, lambda e: e.activation(sq[:], p[:], AF.Square), reads=[pb], writes=[sqb])
                            state[k] = (p, pb, sq, sqb)

                        def stB(k):
                            i, t = items[k]
                            p, pb, sq, sqb = state.pop(k)
                            h = g * 2 + i
                            p2, p2b = PS[4 + k % 2]
                            mm(p2[:], blk_bf[:], sq[:], True, True, [blk_bfb, sqb], [p2b], True)
                            R, Rb = R2.get()
                            rstd_from(p2, p2b, 1.0 / 64, R, Rb)
                            ro, rob = ROW.get()
                            cidx = wcol(l)
                            S_.op(dve, lambda e: e.scalar_tensor_tensor(ro[:], p[:], wv[:, cidx:cidx + 1], R[:],
                                                                        ALU.mult, ALU.mult),
                                  reads=[pb, Rb, wvb], writes=[rob])
                            S_.dma(pool, dst[h, :, t * 512:(t + 1) * 512], ro[:], rob, reads=[rob])

                        for k in range(len(items) + 1):
                            if k < len(items):
                                stA(k)
                            if k >= 1:
                                stB(k - 1)

                for half in range(2):
                    wbs = [WL.load(win_v[:, :, 2048 + half * 512 + g * 256: 2048 + half * 512 + (g + 1) * 256], anw)
                           for g in range(2)]
                    for t in range(NT):
                        vt, vtb = VT.get()
                        for blk in range(4):
                            for g in range(2):
                                k = blk * 2 + g
                                p, pb = PS[k % 4]
                                wb, wbb = wbs[g]
                                for c in range(8):
                                    mm(p[:, 0:256], actT[:, c, t * 512 + blk * 128: t * 512 + (blk + 1) * 128],
                                       wb[:, c, :], c == 0, c == 7, [wbb] + actbufs(c, t), [pb], c == 7)
                                if k % 2 == 0:
                                    S_.op(act, lambda e, p=p, blk=blk, g=g, vt=vt: e.activation(
                                        vt[:, blk, g * 256:(g + 1) * 256], p[:, 0:256], AF.Copy),
                                        reads=[pb], writes=[vtb], partial=True)
                                else:
                                    S_.op(dve, lambda e, p=p, blk=blk, g=g, vt=vt: e.tensor_copy(
                                        vt[:, blk, g * 256:(g + 1) * 256], p[:, 0:256]),
                                        reads=[pb], writes=[vtb], partial=True)
                        for hh in range(4):
                            S_.dma(pool, VHd[half * 4 + hh, :, t * 4:(t + 1) * 4, :], vt[:, :, hh * 128:(hh + 1) * 128],
                                   vtb, reads=[vtb])

                for m, ((wa, wab), (wu, wub)) in pipelined(4, lambda m_: (
                        WL.load(win_v[:, :, 3072 + m_ * 256: 3072 + (m_ + 1) * 256], anw),
                        WL.load(win_v[:, :, 4096 + m_ * 256: 4096 + (m_ + 1) * 256], anw))):
                    for i in range(2):
                        cc = m * 2 + i
                        gl, glb = GLs[cc % 2]
                        dg, dgb = DG.get()
                        for j in range(CW):
                            S_.op(dve, lambda e, j=j, dg=dg, cc=cc: e.tensor_scalar(
                                dg[:, j, :], ident_bf[:], PRB[:, cc, l * CW + j: l * CW + j + 1], None, ALU.mult),
                                reads=[ident_bfb, PRBb], writes=[dgb], track=(j == CW - 1), partial=(j > 0))

                        def stA(t):
                            pa, pab = PS[(2 * t) % 4]
                            pu, pub = PS[(2 * t + 1) % 4]
                            for c in range(8):
                                mm(pa[:], wa[:, c, i * 128:(i + 1) * 128], actT[:, c, t * 512:(t + 1) * 512],
                                   c == 0, c == 7, [wab] + actbufs(c, t), [pab], c == 7)
                            for c in range(8):
                                mm(pu[:], wu[:, c, i * 128:(i + 1) * 128], actT[:, c, t * 512:(t + 1) * 512],
                                   c == 0, c == 7, [wub] + actbufs(c, t), [pub], c == 7)
                            sg, sgb = SG.get()
                            S_.op(act, lambda e: e.activation(sg[:], pu[:], AF.Sigmoid), reads=[pub], writes=[sgb])
                            S_.op(dve, lambda e: e.tensor_tensor(gl[:, 32 + t * 512: 32 + (t + 1) * 512], pa[:], sg[:],
                                                                 ALU.mult),
                                  reads=[pab, sgb], writes=[glb[1 + t]])

                        def stB(t):
                            pc, pcb = PS[4 + t % 2]
                            for j in range(CW):
                                mm(pc[:], dg[:, j, :], gl[:, 2 + j + t * 512: 2 + j + (t + 1) * 512],
                                   j == 0, j == CW - 1, [dgb, glb[t], glb[1 + t]], [pcb], j == CW - 1)
                            co, cob = CO.get()
                            S_.op(act, lambda e: e.activation(co[:], pc[:], AF.Identity, bias=PRA[:, cc, 8 + l: 9 + l]),
                                  reads=[pcb, PRAb], writes=[cob])
                            S_.dma(pool, CTd[cc, :, t * 512:(t + 1) * 512], co[:], cob, reads=[cob])

                        for t in range(NT + 1):
                            if t < NT:
                                stA(t)
                            if t >= 1:
                                stB(t - 1)

                for g, (wb, wbb) in pipelined(8, lambda g_: WL.load(
                        win_v[:, :, 5120 + g_ * 256: 5120 + (g_ + 1) * 256], anw)):
                    for i in range(2):
                        n = g * 2 + i
                        for t in range(NT):
                            p, pb = PS[(n * NT + t) % 4]
                            for c in range(8):
                                mm(p[:], wb[:, c, i * 128:(i + 1) * 128], actT[:, c, t * 512:(t + 1) * 512],
                                   c == 0, c == 7, [wbb] + actbufs(c, t), [pb], c == 7)
                            ro, rob = ROW.get()
                            gcol = 20 + 2 * l + n // 8
                            S_.op(act, lambda e, p=p, ro=ro, n=n, gcol=gcol: e.activation(
                                ro[:], p[:], AF.Sigmoid, bias=PRA[:, n % 8, gcol:gcol + 1]),
                                reads=[pb, PRAb], writes=[rob])
                            S_.dma(pool, Gd[n, :, t * 512:(t + 1) * 512], ro[:], rob, reads=[rob])
                S_.barrier()

            with ExitStack() as ph:
                QTt = TR(ph, "QTt", [128, S], BF16, 2, True)
                KTt = TR(ph, "KTt", [128, S], BF16, 2, True)
                VHt = TR(ph, "VHt", [128, NB, 128], BF16, 2, True)
                TBt = TR(ph, "TBt", [128, 1024], F32, 2, True)
                PT = TR(ph, "PT", [128, 1024], BF16, 6)
                STMP = TR(ph, "stmp", [128, 1024], F32, 2)
                EP = TR(ph, "ep", [128, 512], F32, 4)
                AEP = TR(ph, "aep", [128, 512], F32, 4)
                REP = TR(ph, "rep", [128, 512], F32, 4)
                LNT = TR(ph, "lnt", [128, 512], F32, 4)
                SQ3 = TR(ph, "sq3", [128, 512], BF16, 4)
                OS1 = TR(ph, "os1", [128, 512], F32, 2)
                OS2 = TR(ph, "os2", [128, 512], F32, 2)
                LS1 = TR(ph, "ls1", [128, 512], F32, 2)
                LS2 = TR(ph, "ls2", [128, 512], F32, 2)
                AOUT = TR(ph, "aout", [128, 512], BF16, 2, True) if dbg else None
                sp_rot = Rot(PSPAIR)
                O1, O1b = PS[4]
                O2, O2b = PS[5]
                L1, L1b = PS[6]
                L2, L2b = PS[7]
                DP = 3

                def load_head(h):
                    q = QTt.get(); k = KTt.get(); v = VHt.get(); tb = TBt.get()
                    S_.dma(sp, q[0][:], QTd[h], q[1], writes=[q[1]])
                    S_.dma(sp, k[0][:], KTd[h], k[1], writes=[k[1]])
                    S_.dma(sp, v[0][:], VHd[h], v[1], writes=[v[1]])
                    skew = GRd[h][127:127 + 128 * (GL_L - 1)].rearrange("(p f) -> p f", f=GL_L - 1)[:, 0:1024]
                    S_.dma(sp, tb[0][:], skew, tb[1], writes=[tb[1]])
                    return q, k, v, tb

                deferred = []
                gstep = [0]

                def flush_deferred(force=False):
                    while deferred and (force or deferred[0][0] <= gstep[0]):
                        deferred.pop(0)[1]()

                nxt = load_head(0)
                for h in range(NH):
                    (qt, qtb), (kt, ktb), (vh, vhb), (tbt, tbb) = nxt
                    if h + 1 < NH:
                        nxt = load_head(h + 1)
                    for c in range(NT):
                        nj = 4 * c + 4
                        pend = {}

                        def stA(j):
                            o = 512 * c - 128 * j
                            c0 = max(0, -o)
                            sp_, spb = sp_rot.get()
                            for hf in range(2):
                                mm(sp_[:, hf * 512 + c0:(hf + 1) * 512], kt[hf * 64:(hf + 1) * 64, j * 128:(j + 1) * 128],
                                   qt[hf * 64:(hf + 1) * 64, c * 512 + c0:(c + 1) * 512], True, True, [ktb, qtb], [spb],
                                   hf == 1)
                            pt, ptb = PT.get()
                            spv = sp_.rearrange("p (t q) -> p t q", t=2)
                            ptv = pt[:].rearrange("p (t q) -> p t q", t=2)
                            if o <= 128:
                                tm, tmb = STMP.get()
                                tmv = tm[:].rearrange("p (t q) -> p t q", t=2)
                                for hf in range(2):
                                    S_.op(dve, lambda e, hf=hf: e.tensor_tensor(
                                        tm[:, hf * 512 + c0:(hf + 1) * 512], sp_[:, hf * 512 + c0:(hf + 1) * 512],
                                        tbt[:, o + 384 + c0: o + 384 + 512], ALU.add),
                                        reads=[spb, tbb], writes=[tmb], track=(hf == 1), partial=(hf == 1))
                                S_.op(act, lambda e: e.activation(ptv[:, :, c0:512], tmv[:, :, c0:512], AF.Exp),
                                      reads=[tmb], writes=[ptb])
                            else:
                                S_.op(act, lambda e: e.activation(pt[:], sp_[:], AF.Exp, bias=B31[:, h:h + 1]),
                                      reads=[spb, B31b], writes=[ptb])
                            pend[j] = (pt, ptb, c0)
                            gstep[0] += 1

                        def stB(j):
                            pt, ptb, c0 = pend.pop(j)
                            first, last = (j == 0), (j == nj - 1)
                            for hf, (O, Ob, L, Lb) in enumerate(((O1, O1b, L1, L1b), (O2, O2b, L2, L2b))):
                                mm(O[:, c0:512], vh[:, j, :], pt[:, hf * 512 + c0:(hf + 1) * 512], first, last, [vhb, ptb],
                                   [Ob], last)
                                mm(L[:, c0:512], ones_bf[:], pt[:, hf * 512 + c0:(hf + 1) * 512], first, last,
                                   [ones_bfb, ptb], [Lb], True)

                        for j in range(nj + DP):
                            if j < nj:
                                stA(j)
                                flush_deferred()
                            if j >= DP:
                                stB(j - DP)
                        o1s, o1sb = OS1.get(); o2s, o2sb = OS2.get(); l1s, l1sb = LS1.get(); l2s, l2sb = LS2.get()
                        S_.op(dve, lambda e: e.tensor_copy(o1s[:], O1[:]), reads=[O1b], writes=[o1sb])
                        S_.op(dve, lambda e: e.tensor_copy(l1s[:], L1[:]), reads=[L1b], writes=[l1sb])
                        S_.op(dve, lambda e: e.tensor_copy(o2s[:], O2[:]), reads=[O2b], writes=[o2sb])
                        S_.op(dve, lambda e: e.tensor_copy(l2s[:], L2[:]), reads=[L2b], writes=[l2sb])

                        def part2(o1s=o1s, o1sb=o1sb, o2s=o2s, o2sb=o2sb, l1s=l1s, l1sb=l1sb, l2s=l2s, l2sb=l2sb,
                                  h=h, c=c):
                            m_, mb = EP.get(); u_, ub = EP.get(); v_, vb = EP.get(); w_, wb_ = EP.get()
                            a, ab = AEP.get()
                            S_.op(dve, lambda e: e.tensor_tensor(m_[:], l1s[:], l2s[:], ALU.mult), reads=[l1sb, l2sb], writes=[mb])
                            S_.op(dve, lambda e: e.reciprocal(m_[:], m_[:]), reads=[mb], writes=[mb])
                            S_.op(dve, lambda e: e.tensor_tensor(u_[:], o1s[:], l2s[:], ALU.mult), reads=[o1sb, l2sb], writes=[ub])
                            S_.op(dve, lambda e: e.tensor_tensor(v_[:], o2s[:], l1s[:], ALU.mult), reads=[o2sb, l1sb], writes=[vb])
                            S_.op(dve, lambda e: e.scalar_tensor_tensor(w_[:], v_[:], NLAM[:, l:l + 1], u_[:], ALU.mult, ALU.add),
                                  reads=[ub, vb, NLAMb], writes=[wb_])
                            S_.op(dve, lambda e: e.tensor_tensor(a[:], w_[:], m_[:], ALU.mult), reads=[wb_, mb], writes=[ab])
                            sq, sqb = SQ3.get()
                            S_.op(dve, lambda e: e.tensor_tensor(sq[:], a[:], a[:], ALU.mult), reads=[ab], writes=[sqb])
                            deferred.append([gstep[0] + 8, lambda: part2s(a, ab, sq, sqb, h, c)])

                        def part2s(a, ab, sq, sqb, h, c):
                            ssp, sspb = sp_rot.get()
                            R, Rb = REP.get()
                            lt, ltb = LNT.get()
                            mm(ssp[:, 0:512], ones_bf[:], sq[:], True, True, [ones_bfb, sqb], [sspb], True)
                            S_.op(dve, lambda e: e.tensor_copy(lt[:], ssp[:, 0:512]), reads=[sspb], writes=[ltb])
                            deferred.append([gstep[0] + 3, lambda: part2b(a, ab, R, Rb, lt, ltb, h, c)])

                        def part2b(a, ab, R, Rb, lt, ltb, h, c):
                            S_.op(act, lambda e: e.activation(lt[:], lt[:], AF.Ln, bias=EPS, scale=1.0 / 128),
                                  reads=[ltb], writes=[ltb])
                            S_.op(act, lambda e: e.activation(R[:], lt[:], AF.Exp, scale=-0.5), reads=[ltb], writes=[Rb])
                            S_.op(dve, lambda e: e.scalar_tensor_tensor(
                                actT[:, h, c * 512:(c + 1) * 512], a[:], SL[:, l:l + 1], R[:], ALU.mult, ALU.mult),
                                reads=[ab, Rb, SLb], writes=actbufs(h, c))
                            if dbg:
                                ao, aob = AOUT.get()
                                S_.op(dve, lambda e: e.tensor_copy(ao[:], actT[:, h, c * 512:(c + 1) * 512]),
                                      reads=actbufs(h, c), writes=[aob])
                                S_.dma(pool, ATd[h, :, c * 512:(c + 1) * 512], ao[:], aob, reads=[aob])

                        deferred.append([gstep[0] + 1, part2])
                flush_deferred(True)
                S_.barrier()

            with ExitStack() as ph:
                W4S = TR(ph, "w4_s", [128, 8, 256], F32, 1, True)
                WPA, WPAb = T(ph, "WPA", [128, 8, 1024], BF16)
                WPC, WPCb = T(ph, "WPC", [128, 8, 1024], BF16)
                WOU, WOUb = T(ph, "WOU", [128, 8, 1024], BF16)
                for (wt, wtb, src) in ((WPA, WPAb, w_proj_attn), (WPC, WPCb, w_proj_conv), (WOU, WOUb, w_out)):
                    sv = src[l].rearrange("(c p) n -> p c n", p=128)
                    for g in range(4):
                        sg, sgb = W4S.get()
                        S_.dma(sp, sg[:], sv[:, :, g * 256:(g + 1) * 256], sgb, writes=[sgb])
                        S_.op(dve, lambda e, wt=wt, sg=sg, g=g: e.tensor_copy(wt[:, 0:4, g * 256:(g + 1) * 256], sg[:, 0:4, :]),
                              reads=[sgb], writes=[wtb], partial=True)
                        S_.op(act, lambda e, wt=wt, sg=sg, g=g: e.activation(wt[:, 4:8, g * 256:(g + 1) * 256], sg[:, 4:8, :], AF.Copy),
                              reads=[sgb], writes=[wtb], partial=True)
                X4 = TR(ph, "X4", [128, 8, 256], F32, 2, True)
                C4 = TR(ph, "C4", [128, 8, 256], F32, 2, True)
                G4 = TR(ph, "G4", [128, 16, 256], BF16, 2, True)
                C4b = TR(ph, "C4b", [128, 8, 256], BF16, 1)
                SQ4 = TR(ph, "SQ4", [128, 8, 256], BF16, 1)
                M4 = TR(ph, "M4", [128, 8, 256], BF16, 1)
                E4 = TR(ph, "E4", [128, 256], F32, 8)
                gd_v = Gd.rearrange("n p s -> p n s")
                ct_v = CTd.rearrange("c p s -> p c s")
                x1_v = X1d.rearrange("(c p) s -> p c s", p=128)

                def loads4(s):
                    x = X4.get(); cx = C4.get(); g4 = G4.get()
                    sl = slice(s * 256, (s + 1) * 256)
                    S_.dma(sp, x[0][:], xin_v[:, :, sl], x[1], writes=[x[1]])
                    S_.dma(sp, cx[0][:], ct_v[:, :, sl], cx[1], writes=[cx[1]])
                    S_.dma(sp, g4[0][:], gd_v[:, :, sl], g4[1], writes=[g4[1]])
                    return x, cx, g4

                CN2 = TR(ph, "CN2_", [128, 8, 256], BF16, 2)
                ps1, ps1b = PS[0]
                ps2, ps2b = PS[1]

                def stageA(s, c4, c4b):
                    cb16, cb16b = C4b.get(); sq, sqb = SQ4.get(); cn, cnb = CN2.get()
                    d4, d4b = c4, c4b
                    S_.op(act, lambda e: e.activation(sq[:], c4[:], AF.Square), reads=[c4b], writes=[sqb])
                    S_.op(dve, lambda e: e.tensor_copy(cb16[:], c4[:]), reads=[c4b], writes=[cb16b])
                    for c in range(8):
                        mm(ps1[:, 0:256], ones_bf[:], cb16[:, c, :], c == 0, c == 7, [ones_bfb, cb16b], [ps1b], c == 7)
                    for c in range(8):
                        mm(ps2[:, 0:256], ones_bf[:], sq[:, c, :], c == 0, c == 7, [ones_bfb, sqb], [ps2b], c == 7)
                    mu, mub = E4.get(); msq, msqb = E4.get(); var, varb = E4.get()
                    S_.op(dve, lambda e: e.tensor_scalar(mu[:], ps1[:, 0:256], 1.0 / D, None, ALU.mult),
                          reads=[ps1b], writes=[mub])
                    S_.op(dve, lambda e: e.tensor_tensor(msq[:], mu[:], mu[:], ALU.mult), reads=[mub], writes=[msqb])
                    S_.op(dve, lambda e: e.scalar_tensor_tensor(var[:], ps2[:, 0:256], 1.0 / D, msq[:], ALU.mult,
                                                                ALU.subtract),
                          reads=[ps2b, msqb], writes=[varb])
                    S_.op(act, lambda e: e.activation(var[:], var[:], AF.Sqrt, bias=EPS, scale=1.0),
                          reads=[varb], writes=[varb])
                    S_.op(dve, lambda e: e.reciprocal(var[:], var[:]), reads=[varb], writes=[varb])
                    for c in range(8):
                        S_.op(dve, lambda e, c=c: e.tensor_tensor(d4[:, c, :], c4[:, c, :], mu[:], ALU.subtract),
                              reads=[c4b, mub], writes=[d4b], partial=True)
                        S_.op(dve, lambda e, c=c: e.tensor_tensor(d4[:, c, :], d4[:, c, :], var[:], ALU.mult),
                              reads=[d4b, varb], writes=[d4b], partial=True)
                        S_.op(act, lambda e, c=c: e.activation(
                            cn[:, c, :], d4[:, c, :], AF.Silu, bias=PRA[:, c, 16 + l:17 + l],
                            scale=PRA[:, c, 12 + l:13 + l]),
                            reads=[d4b, PRAb], writes=[cnb], partial=True)
                    return cn, cnb

                def stageB1(s, cn, cnb, g4, g4b):
                    sl = slice(s * 256, (s + 1) * 256)
                    m4, m4b = M4.get()
                    for n in range(8):
                        pa, pab = PS[2 + (n % 2) * 2]
                        pbb_, pbbb = PS[3 + (n % 2) * 2]
                        for c in range(8):
                            mm(pa[:, 0:256], WPA[:, c, n * 128:(n + 1) * 128], actT[:, c, sl], c == 0, c == 7,
                               [WPAb, ACTB[c][s]], [pab], c == 7)
                        for c in range(8):
                            mm(pbb_[:, 0:256], WPC[:, c, n * 128:(n + 1) * 128], cn[:, c, :], c == 0, c == 7,
                               [WPCb, cnb], [pbbb], c == 7)
                        ta, tab_ = E4.get(); tb, tbb_ = E4.get()
                        S_.op(dve, lambda e, n=n, ta=ta, pa=pa: e.tensor_tensor(ta[:], pa[:, 0:256], g4[:, n, :], ALU.mult),
                              reads=[pab, g4b], writes=[tab_])
                        S_.op(dve, lambda e, n=n, tb=tb, pbb_=pbb_: e.tensor_tensor(tb[:], pbb_[:, 0:256], g4[:, 8 + n, :],
                                                                                   ALU.mult),
                              reads=[pbbb, g4b], writes=[tbb_])
                        S_.op(dve, lambda e, n=n, ta=ta, tb=tb: e.tensor_tensor(m4[:, n, :], ta[:], tb[:], ALU.add),
                              reads=[tab_, tbb_], writes=[m4b], partial=True)
                    return m4, m4b

                def stageB2(s, m4, m4b, x4, x4b):
                    sl = slice(s * 256, (s + 1) * 256)
                    for n in range(8):
                        po, pob = PS[6 + n % 2]
                        for c in range(8):
                            mm(po[:, 0:256], WOU[:, c, n * 128:(n + 1) * 128], m4[:, c, :], c == 0, c == 7,
                               [WOUb, m4b], [pob], c == 7)
                        S_.op(dve, lambda e, n=n, po=po: e.tensor_tensor(x4[:, n, :], x4[:, n, :], po[:, 0:256], ALU.add),
                              reads=[pob, x4b], writes=[x4b], partial=True)
                    S_.dma(pool, x1_v[:, :, sl], x4[:], x4b, reads=[x4b])
                    sq2, sq2b = SQ4.get()
                    S_.op(act, lambda e: e.activation(sq2[:], x4[:], AF.Square), reads=[x4b], writes=[sq2b])
                    for c in range(8):
                        mm(ps1[:, 256:512], ones_bf[:], sq2[:, c, :], c == 0, c == 7, [ones_bfb, sq2b], [ps1b], c == 7)
                    R, Rb = E4.get()
                    S_.op(act, lambda e: e.activation(R[:], ps1[:, 256:512], AF.Sqrt, bias=EPS, scale=1.0 / D),
                          reads=[ps1b], writes=[Rb])
                    S_.op(dve, lambda e: e.reciprocal(R[:], R[:]), reads=[Rb], writes=[Rb])
                    for c in range(8):
                        S_.op(dve, lambda e, c=c: e.tensor_tensor(actT[:, c, sl], x4[:, c, :], R[:], ALU.mult),
                              reads=[x4b, Rb], writes=[ACTB[c][s]])

                cur = loads4(0)
                cn_cur = stageA(0, cur[1][0], cur[1][1])
                for s in range(NS):
                    (x4, x4b), _, (g4, g4b) = cur
                    nxt4 = loads4(s + 1) if s + 1 < NS else None
                    m4, m4b = stageB1(s, cn_cur[0], cn_cur[1], g4, g4b)
                    if nxt4 is not None:
                        cn_nxt = stageA(s + 1, nxt4[1][0], nxt4[1][1])
                    stageB2(s, m4, m4b, x4, x4b)
                    if nxt4 is not None:
                        cur, cn_cur = nxt4, cn_nxt
                S_.barrier()

            with ExitStack() as ph:
                WG = WLoader(ph, "wg", 8, 128, 2, 2)
                WU = WLoader(ph, "wu", 8, 128, 2, 2)
                WD = WLoader(ph, "wd", NF, 128, 1, 2)
                HID, _ = T(ph, "HID", [128, NF, FT], BF16)
                HIDB = [[S_.buf(f"hid{f}_{tt}") for tt in range(TCS)] for f in range(NF)]
                SGF = TR(ph, "sgf", [128, 512], F32, 3)
                XF = TR(ph, "xf", [128, 512], F32, 4, True)
                fnw = (PRA[:, :, 4 + l], PRAb)
                wgu_v = w_gate_up[l].rearrange("(c p) n -> p c n", p=128)
                wd_v = w_down[l].rearrange("(f p) n -> p f n", p=128)
                for p_ in range(NP):
                    for f, ((wg, wgb), (wu, wub)) in pipelined(NF, lambda f_: (
                            WG.load(wgu_v[:, :, f_ * 128:(f_ + 1) * 128], fnw),
                            WU.load(wgu_v[:, :, FF + f_ * 128: FF + (f_ + 1) * 128], fnw))):
                        for tt in range(TCS):
                            t = p_ * TCS + tt
                            k = f * TCS + tt
                            pg, pgb = PS[(2 * k) % 4]
                            pu, pub = PS[(2 * k + 1) % 4]
                            for c in range(8):
                                mm(pg[:], wg[:, c, :], actT[:, c, t * 512:(t + 1) * 512], c == 0, c == 7,
                                   [wgb] + actbufs(c, t), [pgb], c == 7)
                            for c in range(8):
                                mm(pu[:], wu[:, c, :], actT[:, c, t * 512:(t + 1) * 512], c == 0, c == 7,
                                   [wub] + actbufs(c, t), [pub], c == 7)
                            sg, sgb = SGF.get()
                            S_.op(act, lambda e, sg=sg, pg=pg: e.activation(sg[:], pg[:], AF.Silu), reads=[pgb], writes=[sgb])
                            S_.op(dve, lambda e, sg=sg, pu=pu, f=f, tt=tt: e.tensor_tensor(
                                HID[:, f, tt * 512:(tt + 1) * 512], sg[:], pu[:], ALU.mult),
                                reads=[sgb, pub], writes=[HIDB[f][tt]])
                    for n, (wd, wdb) in pipelined(8, lambda n_: WD.load(wd_v[:, :, n_ * 128:(n_ + 1) * 128])):
                        for tt in range(TCS):
                            t = p_ * TCS + tt
                            xf, xfb = XF.get()
                            S_.dma(sp, xf[:], X1d[n * 128:(n + 1) * 128, t * 512:(t + 1) * 512], xfb, writes=[xfb])
                            pd, pdb = PS[4 + (n * TCS + tt) % 4]
                            for f in range(NF):
                                mm(pd[:], wd[:, f, :], HID[:, f, tt * 512:(tt + 1) * 512], f == 0, f == NF - 1,
                                   [wdb, HIDB[f][tt]], [pdb], f == NF - 1)
                            S_.op(dve, lambda e, xf=xf, pd=pd: e.tensor_tensor(xf[:], xf[:], pd[:], ALU.add),
                                  reads=[xfb, pdb], writes=[xfb])
                            S_.dma(pool, xout[n * 128:(n + 1) * 128, t * 512:(t + 1) * 512], xf[:], xfb, reads=[xfb])
                S_.barrier()
        S_.emit()
    return nc


_NC_CACHE = {}


def kernel(x, attn_norm_w, w_in, q_norm_w, k_norm_w, lam_q1, lam_k1, lam_q2, lam_k2,
           subln_w, w_proj_attn, conv_w, conv_b, conv_ln_w, conv_ln_b, w_proj_conv,
           gate_b, w_out, ffn_norm_w, w_gate_up, w_down, rel_bias):
    x = np.asarray(x)
    B, S, _ = x.shape
    if "nc" not in _NC_CACHE:
        _NC_CACHE["nc"] = build(S, 4)
    nc = _NC_CACHE["nc"]
    f = lambda a: np.ascontiguousarray(np.asarray(a, dtype=np.float32))
    shared = dict(attn_norm_w=f(attn_norm_w), w_in=f(w_in), q_norm_w=f(q_norm_w), k_norm_w=f(k_norm_w),
                  lam_q1=f(lam_q1), lam_k1=f(lam_k1), lam_q2=f(lam_q2), lam_k2=f(lam_k2), subln_w=f(subln_w),
                  w_proj_attn=f(w_proj_attn), conv_w=f(conv_w), conv_b=f(conv_b), conv_ln_w=f(conv_ln_w),
                  conv_ln_b=f(conv_ln_b), w_proj_conv=f(w_proj_conv), gate_b=f(gate_b), w_out=f(w_out),
                  ffn_norm_w=f(ffn_norm_w), w_gate_up=f(w_gate_up), w_down=f(w_down), rel_bias=f(rel_bias),
                  oh=onehot_table())
    in_maps = []
    for b in range(B):
        m = dict(shared)
        m["xT"] = np.ascontiguousarray(x[b].T)
        in_maps.append(m)
    res = run_bass_kernel_spmd(nc, in_maps, core_ids=list(range(B)))
    out = np.stack([np.ascontiguousarray(res.results[b]["outT"].T) for b in range(B)], axis=0)
    return out.astype(np.float32)
```

```python
import math
import numpy as np
from contextlib import ExitStack
import concourse.bass as bass
import concourse.mybir as mybir
from concourse.bass_utils import run_bass_kernel_spmd

F32 = mybir.dt.float32
BF16 = mybir.dt.bfloat16
AF = mybir.ActivationFunctionType
ALU = mybir.AluOpType
AX = mybir.AxisListType

D = 1024
NH = 8
FF = 2816
NF = FF // 128
CW = 31
INC = 7168
EPS = 1e-6
NEG = -30000.0
GL_L = 1152


class DSem:
    def __init__(self, sem):
        self.sem = sem
        self.count = 0


class Buf:
    __slots__ = ("name", "writers", "readers", "dsem", "prev_readers", "prev_writers")

    def __init__(self, name, dsem=None):
        self.name = name
        self.writers = {}
        self.readers = {}
        self.prev_readers = {}
        self.prev_writers = {}
        self.dsem = dsem


class Eng:
    def __init__(self, name):
        self.name = name
        self.sem = None
        self.count = 0
        self.ops = []
        self.waited = {}
        self.pending = []


class Sched:
    def __init__(self, nc, stack):
        self.nc = nc
        self.stack = stack
        self.pe = Eng("tensor")
        self.act = Eng("scalar")
        self.dve = Eng("vector")
        self.pool = Eng("gpsimd")
        self.sp = Eng("sync")
        self.engs = [self.pe, self.act, self.dve, self.pool, self.sp]
        for e in self.engs:
            e.sem = stack.enter_context(nc.semaphore("es_" + e.name))
        self.dsems = []
        self.dcache = {}

    def dsem(self, name):
        if name in self.dcache:
            return self.dcache[name]
        d = DSem(self.stack.enter_context(self.nc.semaphore("ds_" + name)))
        self.dsems.append(d)
        self.dcache[name] = d
        return d

    def buf(self, name, dma=False):
        return Buf(name, self.dsem(name) if dma else None)

    def _resolve(self, tok, eng):
        if tok[0] == "e":
            if tok[1] is self.pe and eng is self.pe:
                return None
            assert tok[2] is not None, "unresolved token (mark producer tracked)"
            return (tok[1].sem, tok[2])
        d = tok[1]
        return (d.sem, 16 * d.count)

    def _waits(self, eng, reads, writes, partial):
        waits = {}
        toks = []
        for b in reads:
            toks.extend(b.writers.values())
        for b in writes:
            toks.extend(b.readers.values())
            if not partial:
                toks.extend(b.writers.values())
            else:
                toks.extend(b.prev_readers.values())
                toks.extend(b.prev_writers.values())
        for t in toks:
            r = self._resolve(t, eng)
            if r is None:
                continue
            sem, val = r
            k = id(sem)
            if eng.waited.get(k, 0) < val:
                eng.waited[k] = val
                waits[k] = (sem, val)
        return list(waits.values())

    def _register(self, tok, reads, writes, partial):
        key = (tok[0], id(tok[1]))
        for b in reads:
            b.readers[key] = tok
        for b in writes:
            if partial:
                b.writers[key] = tok
            else:
                b.prev_readers = b.readers
                b.prev_writers = b.writers
                b.writers = {key: tok}
                b.readers = {}

    def op(self, eng, fn, reads=(), writes=(), track=True, partial=False):
        r = Rec()
        fn(r)
        name, a, k = r.call
        fn = lambda e: getattr(e, name)(*a, **k)
        waits = self._waits(eng, reads, writes, partial)
        if track:
            eng.count += 1
            tok = ["e", eng, eng.count]
            for p in eng.pending:
                p[2] = eng.count
            eng.pending = []
            eng.ops.append((waits, fn, (eng.sem, 1)))
        else:
            tok = ["e", eng, None]
            eng.pending.append(tok)
            eng.ops.append((waits, fn, None))
        self._register(tok, reads, writes, partial)

    def dma(self, eng, out, in_, dbuf, reads=(), writes=(), partial=False):
        waits = self._waits(eng, reads, writes, partial)
        d = dbuf.dsem
        d.count += 1
        tok = ["d", d, d.count]
        eng.ops.append((waits, lambda e: e.dma_start(out=out, in_=in_), (d.sem, 16)))
        self._register(tok, reads, writes, partial)

    def barrier(self):
        targets = []
        for d in self.dsems:
            if d.count:
                targets.append((d.sem, 16 * d.count))
        for e in self.engs:
            if e.count:
                assert not e.pending, "pending untracked ops at barrier on " + e.name
                targets.append((e.sem, e.count))
        for e in self.engs:
            waits = []
            for sem, val in targets:
                if sem is e.sem and e is not self.pe and False:
                    continue
                k = id(sem)
                if e.waited.get(k, 0) < val:
                    e.waited[k] = val
                    waits.append((sem, val))
            e.ops.append((waits, None, None))

    def emit(self):
        nc = self.nc
        block = self.stack.enter_context(nc.Block())

        def run(eng):
            def f(e):
                for waits, fn, inc in eng.ops:
                    for sem, val in waits:
                        e.wait_ge(sem, val)
                    if fn is not None:
                        ins = fn(e)
                        if inc is not None:
                            ins.then_inc(inc[0], inc[1])
            return f

        block.tensor(run(self.pe))
        block.scalar(run(self.act))
        block.vector(run(self.dve))
        block.gpsimd(run(self.pool))
        block.sync(run(self.sp))


class Rec:
    def __init__(self):
        self.call = None

    def __getattr__(self, name):
        def f(*a, **k):
            self.call = (name, a, k)
            return self
        return f


class Rot:
    def __init__(self, items):
        self.items = items
        self.i = 0

    def get(self):
        it = self.items[self.i % len(self.items)]
        self.i += 1
        return it


def lam_init(l):
    return 0.8 - 0.6 * math.exp(-0.3 * l)


def t5_bucket_np(n):
    n = np.maximum(n, 0)
    nf = np.maximum(n, 1).astype(np.float32)
    large = 16 + (np.log(nf / np.float32(16)) / np.float32(math.log(128 / 16)) * np.float32(16)).astype(np.int32)
    large = np.minimum(large, 31)
    return np.where(n < 16, n, large)


def onehot_table():
    oh = np.zeros((33, GL_L), np.float32)
    m = np.arange(GL_L)
    b = t5_bucket_np((m - 511).astype(np.int32))
    for i in range(GL_L):
        if i < 511:
            oh[32, i] = 1.0
        else:
            oh[b[i], i] = 1.0
    return oh


def build(S=4096, DEPTH=4, dbg=False):
    NT = S // 512
    NS = S // 256
    NB = S // 128
    FT = min(1024, S)
    NP = S // FT
    TCS = FT // 512
    nc = bass.Bass("TRN2", target_bir_lowering=False)
    okind = "ExternalOutput" if dbg else "Internal"

    def din(name, shape):
        return nc.dram_tensor(name, list(shape), F32, kind="ExternalInput").ap()

    xT = din("xT", [D, S])
    attn_norm_w = din("attn_norm_w", [DEPTH, D])
    w_in = din("w_in", [DEPTH, D, INC])
    q_norm_w = din("q_norm_w", [DEPTH, 64])
    k_norm_w = din("k_norm_w", [DEPTH, 64])
    lam_in = [din(n, [DEPTH, 64]) for n in ("lam_q1", "lam_k1", "lam_q2", "lam_k2")]
    subln_w = din("subln_w", [DEPTH, 128])
    w_proj_attn = din("w_proj_attn", [DEPTH, D, D])
    conv_w = din("conv_w", [DEPTH, CW, D])
    conv_b = din("conv_b", [DEPTH, D])
    conv_ln_w = din("conv_ln_w", [DEPTH, D])
    conv_ln_b = din("conv_ln_b", [DEPTH, D])
    w_proj_conv = din("w_proj_conv", [DEPTH, D, D])
    gate_b = din("gate_b", [DEPTH, 2 * D])
    w_out = din("w_out", [DEPTH, D, D])
    ffn_norm_w = din("ffn_norm_w", [DEPTH, D])
    w_gate_up = din("w_gate_up", [DEPTH, D, 2 * FF])
    w_down = din("w_down", [DEPTH, FF, D])
    rel_bias = din("rel_bias", [32, NH])
    oh_d = din("oh", [33, GL_L])
    outT = nc.dram_tensor("outT", [D, S], F32, kind="ExternalOutput").ap()

    QTd = nc.dram_tensor("QTd", [NH, 128, S], BF16, kind=okind).ap()
    KTd = nc.dram_tensor("KTd", [NH, 128, S], BF16, kind=okind).ap()
    VHd = nc.dram_tensor("VHd", [NH, 128, NB, 128], BF16, kind=okind).ap()
    Gd = nc.dram_tensor("Gd", [16, 128, S], BF16, kind=okind).ap()
    CTd = nc.dram_tensor("CTd", [8, 128, S], F32, kind=okind).ap()
    X1d = nc.dram_tensor("X1d", [D, S], F32, kind=okind).ap()
    X2d = nc.dram_tensor("X2d", [D, S], F32, kind=okind).ap()
    GRd = nc.dram_tensor("GRd", [NH, 128 * GL_L], F32, kind=okind).ap()
    ATd = nc.dram_tensor("ATd", [8, 128, S], BF16, kind=okind).ap() if dbg else None

    with ExitStack() as st:
        S_ = Sched(nc, st)
        pe, act, dve, pool, sp = S_.pe, S_.act, S_.dve, S_.pool, S_.sp

        uid = [0]

        def T(stack, name, shape, dt, dma=False):
            uid[0] += 1
            t = stack.enter_context(nc.sbuf_tensor(f"{name}_u{uid[0]}", list(shape), dt))
            return t, S_.buf(name, dma)

        def TR(stack, name, shape, dt, n, dma=False):
            return Rot([T(stack, f"{name}{i}", shape, dt, dma) for i in range(n)])

        actT, _ = T(st, "actT", [128, 8, S], BF16)
        ACTB = [[S_.buf(f"act{c}_{s}") for s in range(NS)] for c in range(8)]
        ones_bf, ones_bfb = T(st, "ones_bf", [128, 128], BF16)
        blk_bf, blk_bfb = T(st, "blk_bf", [128, 128], BF16)
        ident_f, ident_fb = T(st, "ident_f", [128, 128], F32)
        ident_bf, ident_bfb = T(st, "ident_bf", [128, 128], BF16)
        PRA, PRAb = T(st, "PRA", [128, 8, 28], F32)
        PRB, PRBb = T(st, "PRB", [128, 8, 4 * CW], F32)
        PRC, PRCb = T(st, "PRC", [128, 12], F32)
        QS, QSb = T(st, "QS", [128, 4], F32)
        SL, SLb = T(st, "SL", [128, 4], F32)
        NLAM, NLAMb = T(st, "NLAM", [128, 4], F32)
        B31, B31b = T(st, "B31", [128, NH], F32)
        PSALL = st.enter_context(nc.psum_tensor("psall", [128, 4096], F32))
        PS = [(PSALL[:, i * 512:(i + 1) * 512], S_.buf(f"ps{i}")) for i in range(8)]
        PSPAIR = [(PSALL[:, i * 1024:(i + 1) * 1024], S_.buf(f"pspair{i}")) for i in range(2)]

        def mm(out, lhsT, rhs, start, stop, reads, writes, track):
            S_.op(pe, lambda e: e.matmul(out, lhsT, rhs, start=start, stop=stop),
                  reads=reads, writes=writes, track=track, partial=not start)

        with ExitStack() as ss:
            ones_f, ones_fb = T(ss, "ones_f", [128, 128], F32)
            PA, PAb = T(ss, "PA", [28, 1024], F32, True)
            PB, PBb = T(ss, "PB", [4 * CW, 1024], F32, True)
            PC, PCb = T(ss, "PC", [12, 128], F32, True)
            LV = [T(ss, f"LV{i}", [128, 4 * 64], F32, True) for i in range(4)]
            TAB, TABb = T(ss, "TAB", [33, NH], F32, True)
            OHs, OHb = T(ss, "OHs", [33, GL_L], F32, True)
            LT, LTb = T(ss, "LT", [33, 128], F32)
            GR = TR(ss, "GR", [128, GL_L], F32, 2, True)
            lt1, lt1b = T(ss, "lt1", [128, 256], F32)
            lt2, lt2b = T(ss, "lt2", [128, 8], F32)

            S_.op(pool, lambda e: e.memset(ones_f[:], 1.0), writes=[ones_fb])
            S_.op(pool, lambda e: e.memset(ones_bf[:], 1.0), writes=[ones_bfb])
            S_.op(pool, lambda e: e.memset(ident_f[:], 1.0), writes=[ident_fb])
            S_.op(pool, lambda e: e.affine_select(ident_f[:], ident_f[:], [[-1, 128]], ALU.is_equal, 0.0,
                                                  base=0, channel_multiplier=1),
                  reads=[ident_fb], writes=[ident_fb])
            S_.op(dve, lambda e: e.tensor_copy(ident_bf[:], ident_f[:]), reads=[ident_fb], writes=[ident_bfb])
            S_.op(pool, lambda e: e.memset(blk_bf[:], 0.0), writes=[blk_bfb])
            S_.op(pool, lambda e: e.memset(blk_bf[0:64, 0:64], 1.0), reads=[blk_bfb], writes=[blk_bfb])
            S_.op(pool, lambda e: e.memset(blk_bf[64:128, 64:128], 1.0), reads=[blk_bfb], writes=[blk_bfb])

            for i, src in enumerate((attn_norm_w, ffn_norm_w, conv_b, conv_ln_w, conv_ln_b)):
                S_.dma(sp, PA[4 * i:4 * i + DEPTH, :], src, PAb, writes=[PAb], partial=True)
            S_.dma(sp, PA[20:20 + 2 * DEPTH, :], gate_b.rearrange("l (h n) -> (l h) n", h=2), PAb, writes=[PAb],
                   partial=True)
            S_.dma(sp, PB[0:DEPTH * CW, :], conv_w.rearrange("l j n -> (l j) n"), PBb, writes=[PBb])
            S_.dma(sp, PC[0:DEPTH, 0:64], q_norm_w, PCb, writes=[PCb], partial=True)
            S_.dma(sp, PC[0:DEPTH, 64:128], q_norm_w, PCb, writes=[PCb], partial=True)
            S_.dma(sp, PC[4:4 + DEPTH, 0:64], k_norm_w, PCb, writes=[PCb], partial=True)
            S_.dma(sp, PC[4:4 + DEPTH, 64:128], k_norm_w, PCb, writes=[PCb], partial=True)
            S_.dma(sp, PC[8:8 + DEPTH, :], subln_w, PCb, writes=[PCb], partial=True)
            for i in range(4):
                S_.dma(sp, LV[i][0][:, 0:DEPTH * 64],
                       lam_in[i].rearrange("l d -> (l d)").rearrange("(o n) -> o n", o=1).partition_broadcast(128),
                       LV[i][1], writes=[LV[i][1]])
            S_.dma(sp, TAB[0:32, :], rel_bias, TABb, writes=[TABb], partial=True)
            S_.op(pool, lambda e: e.memset(TAB[32:33, :], NEG), writes=[TABb], partial=True)
            S_.dma(sp, OHs[:], oh_d, OHb, writes=[OHb])

            nrA = 20 + 2 * DEPTH
            p0, p0b = PS[0]
            for cc in range(8):
                S_.op(pe, lambda e, cc=cc: e.transpose(p0[:, cc * 28:cc * 28 + nrA], PA[0:nrA, cc * 128:(cc + 1) * 128],
                                                       ident_f[0:nrA, 0:nrA]),
                      reads=[PAb, ident_fb], writes=[p0b], track=(cc == 7), partial=True)
            S_.op(dve, lambda e: e.tensor_copy(PRA[:].rearrange("p c r -> p (c r)"), p0[:, 0:224]),
                  reads=[p0b], writes=[PRAb])
            nrB = DEPTH * CW
            for half in range(2):
                ph, phb = PS[1 + half]
                for k in range(4):
                    cc = half * 4 + k
                    S_.op(pe, lambda e, cc=cc, k=k, ph=ph: e.transpose(
                        ph[:, k * 124:k * 124 + nrB], PB[0:nrB, cc * 128:(cc + 1) * 128], ident_f[0:nrB, 0:nrB]),
                        reads=[PBb, ident_fb], writes=[phb], track=(k == 3), partial=True)
                S_.op(dve, lambda e, half=half, ph=ph: e.tensor_copy(
                    PRB[:, half * 4:half * 4 + 4, :].rearrange("p c r -> p (c r)"), ph[:, 0:496]),
                    reads=[phb], writes=[PRBb], partial=True)
            p3, p3b = PS[3]
            nrC = 8 + DEPTH
            S_.op(pe, lambda e: e.transpose(p3[:, 0:nrC], PC[0:nrC, :], ident_f[0:nrC, 0:nrC]),
                  reads=[PCb, ident_fb], writes=[p3b])
            S_.op(dve, lambda e: e.tensor_copy(PRC[:, 0:nrC], p3[:, 0:nrC]), reads=[p3b], writes=[PRCb])
            S_.op(dve, lambda e: e.tensor_scalar(QS[:, 0:DEPTH], PRC[:, 0:DEPTH], 0.125, None, ALU.mult),
                  reads=[PRCb], writes=[QSb])
            for l in range(DEPTH):
                S_.op(dve, lambda e, l=l: e.tensor_scalar(SL[:, l:l + 1], PRC[:, 8 + l:9 + l], 1.0 - lam_init(l), None,
                                                          ALU.mult),
                      reads=[PRCb], writes=[SLb], partial=True)
            for k in range(2):
                a_t, a_b = LV[2 * k]
                b_t, b_b = LV[2 * k + 1]
                S_.op(dve, lambda e, a_t=a_t, b_t=b_t: e.tensor_tensor(lt1[:, 0:DEPTH * 64], a_t[:, 0:DEPTH * 64],
                                                                     b_t[:, 0:DEPTH * 64], ALU.mult),
                      reads=[a_b, b_b], writes=[lt1b])
                S_.op(dve, lambda e, k=k: e.tensor_reduce(lt2[:, 4 * k:4 * k + DEPTH],
                                                          lt1[:, 0:DEPTH * 64].rearrange("p (l d) -> p l d", d=64),
                                                          AX.X, ALU.add),
                      reads=[lt1b], writes=[lt2b], partial=True)
            S_.op(act, lambda e: e.activation(lt2[:, 0:8], lt2[:, 0:8], AF.Exp), reads=[lt2b], writes=[lt2b]) \
                if DEPTH == 4 else [S_.op(act, lambda e, k=k: e.activation(lt2[:, 4 * k:4 * k + DEPTH],
                                                                         lt2[:, 4 * k:4 * k + DEPTH], AF.Exp),
                                          reads=[lt2b], writes=[lt2b]) for k in range(2)]
            S_.op(dve, lambda e: e.tensor_tensor(NLAM[:, 0:DEPTH], lt2[:, 4:4 + DEPTH], lt2[:, 0:DEPTH], ALU.subtract),
                  reads=[lt2b], writes=[NLAMb])
            for l in range(DEPTH):
                S_.op(dve, lambda e, l=l: e.tensor_scalar(NLAM[:, l:l + 1], NLAM[:, l:l + 1], -lam_init(l), None,
                                                          ALU.add),
                      reads=[NLAMb], writes=[NLAMb])
            for h in range(NH):
                S_.op(dve, lambda e, h=h: e.tensor_scalar(LT[:, :], ones_f[0:33, :], TAB[0:33, h:h + 1], None, ALU.mult),
                      reads=[ones_fb, TABb], writes=[LTb])
                gr, grb = GR.get()
                for k, (c0, cn) in enumerate(((0, 512), (512, 512), (1024, GL_L - 1024))):
                    pk, pkb = PS[4 + k]
                    S_.op(pe, lambda e, pk=pk, c0=c0, cn=cn: e.matmul(pk[:, 0:cn], LT[:, :], OHs[:, c0:c0 + cn],
                                                                      start=True, stop=True),
                          reads=[LTb, OHb], writes=[pkb])
                    S_.op(act if k == 1 else dve,
                          (lambda e, pk=pk, c0=c0, cn=cn, gr=gr: e.activation(gr[:, c0:c0 + cn], pk[:, 0:cn], AF.Copy))
                          if k == 1 else
                          (lambda e, pk=pk, c0=c0, cn=cn, gr=gr: e.tensor_copy(gr[:, c0:c0 + cn], pk[:, 0:cn])),
                          reads=[pkb], writes=[grb], partial=True)
                S_.op(dve, lambda e, h=h, gr=gr: e.tensor_copy(B31[:, h:h + 1], gr[:, 1150:1151]),
                      reads=[grb], writes=[B31b], partial=True)
                S_.dma(pool, GRd[h].rearrange("(p f) -> p f", f=GL_L), gr[:], grb, reads=[grb])
            S_.barrier()

        class WLoader:
            def __init__(self, stack, name, C, ncols, nstg, nwb):
                self.C, self.ncols = C, ncols
                self.stg = TR(stack, name + "_s", [128, C, ncols], F32, nstg, True)
                self.wb = TR(stack, name + "_w", [128, C, ncols], BF16, nwb)

            def load(self, src, scale=None, ncols=None):
                n = self.ncols if ncols is None else ncols
                sg, sgb = self.stg.get()
                wb, wbb = self.wb.get()
                S_.dma(sp, sg[:, :, 0:n], src, sgb, writes=[sgb])
                C = self.C
                hc = C // 2
                if scale is None:
                    S_.op(dve, lambda e: e.tensor_copy(wb[:, 0:hc, 0:n], sg[:, 0:hc, 0:n]), reads=[sgb], writes=[wbb])
                    S_.op(act, lambda e: e.activation(wb[:, hc:C, 0:n], sg[:, hc:C, 0:n], AF.Copy),
                          reads=[sgb], writes=[wbb], partial=True)
                else:
                    sc, scb = scale
                    for c in range(C):
                        if c % 2 == 0:
                            S_.op(dve, lambda e, c=c: e.tensor_scalar(wb[:, c, 0:n], sg[:, c, 0:n], sc[:, c:c + 1], None,
                                                                      ALU.mult),
                                  reads=[sgb, scb], writes=[wbb], track=(c >= C - 2), partial=(c > 0))
                        else:
                            S_.op(act, lambda e, c=c: e.activation(wb[:, c, 0:n], sg[:, c, 0:n], AF.Identity,
                                                                   scale=sc[:, c:c + 1]),
                                  reads=[sgb, scb], writes=[wbb], track=(c >= C - 2), partial=True)
                return wb, wbb

        def pipelined(n, load):
            nxt = load(0)
            for i in range(n):
                cur = nxt
                if i + 1 < n:
                    nxt = load(i + 1)
                yield i, cur

        def rstd_from(ps_t, ps_b, n_inv, R, Rb, w=512):
            S_.op(act, lambda e: e.activation(R[:, 0:w], ps_t[:, 0:w], AF.Ln, bias=EPS, scale=n_inv),
                  reads=[ps_b], writes=[Rb])
            S_.op(act, lambda e: e.activation(R[:, 0:w], R[:, 0:w], AF.Exp, scale=-0.5), reads=[Rb], writes=[Rb])

        def actbufs(c, t):
            return [ACTB[c][2 * t], ACTB[c][2 * t + 1]]

        import os
        for l in range(DEPTH):
            if l >= int(os.environ.get('KSTOP', '99')):
                break
            xin = xT if l == 0 else X2d
            xout = outT if l == DEPTH - 1 else X2d
            xin_v = xin.rearrange("(c p) s -> p c s", p=128)
            win_v = w_in[l].rearrange("(c p) n -> p c n", p=128)

            with ExitStack() as ph:
                XC = TR(ph, "XC", [128, 8, 512], F32, 2, True)
                SQ = TR(ph, "SQ1", [128, 8, 512], BF16, 2)
                RR = TR(ph, "R1", [128, 512], F32, 2)
                for t in range(NT):
                    xc, xcb = XC.get()
                    sq, sqb = SQ.get()
                    R, Rb = RR.get()
                    S_.dma(sp, xc[:], xin_v[:, :, t * 512:(t + 1) * 512], xcb, writes=[xcb])
                    S_.op(act, lambda e, xc=xc, sq=sq: e.activation(sq[:], xc[:], AF.Square), reads=[xcb], writes=[sqb])
                    p, pb = PS[t % 2]
                    for c in range(8):
                        mm(p[:], ones_bf[:], sq[:, c, :], c == 0, c == 7, [sqb, ones_bfb], [pb], c == 7)
                    rstd_from(p, pb, 1.0 / D, R, Rb)
                    for c in range(8):
                        eng = dve
                        S_.op(eng, lambda e, c=c, xc=xc, R=R, t=t: e.tensor_tensor(
                            actT[:, c, t * 512:(t + 1) * 512], xc[:, c, :], R[:], ALU.mult),
                            reads=[xcb, Rb], writes=actbufs(c, t))
                S_.barrier()

            with ExitStack() as ph:
                WL = WLoader(ph, "wi", 8, 256, 2, 4)
                anw = (PRA[:, :, l], PRAb)
                SQb = TR(ph, "sq2", [128, 512], BF16, 3)
                R2 = TR(ph, "R2", [128, 512], F32, 3)
                ROW = TR(ph, "row", [128, 512], BF16, 4, True)
                VT = TR(ph, "vt", [128, 4, 512], BF16, 2, True)
                GLs = []
                for i in range(2):
                    g, _ = T(ph, f"GL{i}", [128, 32 + S], BF16)
                    gb_ = [S_.buf(f"GL{i}_pad")] + [S_.buf(f"GL{i}_{t}") for t in range(NT)]
                    S_.op(pool, lambda e, g=g: e.memset(g[:, 0:32], 0.0), writes=[gb_[0]])
                    GLs.append((g, gb_))
                DG = TR(ph, "dg", [128, CW, 128], BF16, 2)
                SG = TR(ph, "sg", [128, 512], F32, 2)
                CO = TR(ph, "co", [128, 512], F32, 3, True)

                for sec in range(2):
                    dst = QTd if sec == 0 else KTd
                    wv = (QS if sec == 0 else PRC)
                    wvb = (QSb if sec == 0 else PRCb)
                    wcol = (lambda l_: l_) if sec == 0 else (lambda l_: 4 + l_)
                    for g, (wb, wbb) in pipelined(4, lambda g_: WL.load(
                            win_v[:, :, sec * 1024 + g_ * 256: sec * 1024 + (g_ + 1) * 256], anw)):
                        items = [(i, t) for i in range(2) for t in range(NT)]
                        state = {}

                        def stA(k):
                            i, t = items[k]
                            p, pb = PS[k % 4]
                            for c in range(8):
                                mm(p[:], wb[:, c, i * 128:(i + 1) * 128], actT[:, c, t * 512:(t + 1) * 512],
                                   c == 0, c == 7, [wbb] + actbufs(c, t), [pb], c == 7)
                            sq, sqb = SQb.get()
                            S_.op(act, lambda e: e.activation(sq[:], p[:], AF.Square), reads=[pb], writes=[sqb])
                            state[k] = (p, pb, sq, sqb)

                        def stB(k):
                            i, t = items[k]
                            p, pb, sq, sqb = state.pop(k)
                            h = g * 2 + i
                            p2, p2b = PS[4 + k % 2]
                            mm(p2[:], blk_bf[:], sq[:], True, True, [blk_bfb, sqb], [p2b], True)
                            R, Rb = R2.get()
                            rstd_from(p2, p2b, 1.0 / 64, R, Rb)
                            ro, rob = ROW.get()
                            cidx = wcol(l)
                            S_.op(dve, lambda e: e.scalar_tensor_tensor(ro[:], p[:], wv[:, cidx:cidx + 1], R[:],
                                                                        ALU.mult, ALU.mult),
                                  reads=[pb, Rb, wvb], writes=[rob])
                            S_.dma(pool, dst[h, :, t * 512:(t + 1) * 512], ro[:], rob, reads=[rob])

                        for k in range(len(items) + 1):
                            if k < len(items):
                                stA(k)
                            if k >= 1:
                                stB(k - 1)

                for half in range(2):
                    wbs = [WL.load(win_v[:, :, 2048 + half * 512 + g * 256: 2048 + half * 512 + (g + 1) * 256], anw)
                           for g in range(2)]
                    for t in range(NT):
                        vt, vtb = VT.get()
                        for blk in range(4):
                            for g in range(2):
                                k = blk * 2 + g
                                p, pb = PS[k % 4]
                                wb, wbb = wbs[g]
                                for c in range(8):
                                    mm(p[:, 0:256], actT[:, c, t * 512 + blk * 128: t * 512 + (blk + 1) * 128],
                                       wb[:, c, :], c == 0, c == 7, [wbb] + actbufs(c, t), [pb], c == 7)
                                if k % 2 == 0:
                                    S_.op(act, lambda e, p=p, blk=blk, g=g, vt=vt: e.activation(
                                        vt[:, blk, g * 256:(g + 1) * 256], p[:, 0:256], AF.Copy),
                                        reads=[pb], writes=[vtb], partial=True)
                                else:
                                    S_.op(dve, lambda e, p=p, blk=blk, g=g, vt=vt: e.tensor_copy(
                                        vt[:, blk, g * 256:(g + 1) * 256], p[:, 0:256]),
                                        reads=[pb], writes=[vtb], partial=True)
                        for hh in range(4):
                            S_.dma(pool, VHd[half * 4 + hh, :, t * 4:(t + 1) * 4, :], vt[:, :, hh * 128:(hh + 1) * 128],
                                   vtb, reads=[vtb])

                for m, ((wa, wab), (wu, wub)) in pipelined(4, lambda m_: (
                        WL.load(win_v[:, :, 3072 + m_ * 256: 3072 + (m_ + 1) * 256], anw),
                        WL.load(win_v[:, :, 4096 + m_ * 256: 4096 + (m_ + 1) * 256], anw))):
                    for i in range(2):
                        cc = m * 2 + i
                        gl, glb = GLs[cc % 2]
                        dg, dgb = DG.get()
                        for j in range(CW):
                            S_.op(dve, lambda e, j=j, dg=dg, cc=cc: e.tensor_scalar(
                                dg[:, j, :], ident_bf[:], PRB[:, cc, l * CW + j: l * CW + j + 1], None, ALU.mult),
                                reads=[ident_bfb, PRBb], writes=[dgb], track=(j == CW - 1), partial=(j > 0))

                        def stA(t):
                            pa, pab = PS[(2 * t) % 4]
                            pu, pub = PS[(2 * t + 1) % 4]
                            for c in range(8):
                                mm(pa[:], wa[:, c, i * 128:(i + 1) * 128], actT[:, c, t * 512:(t + 1) * 512],
                                   c == 0, c == 7, [wab] + actbufs(c, t), [pab], c == 7)
                            for c in range(8):
                                mm(pu[:], wu[:, c, i * 128:(i + 1) * 128], actT[:, c, t * 512:(t + 1) * 512],
                                   c == 0, c == 7, [wub] + actbufs(c, t), [pub], c == 7)
                            sg, sgb = SG.get()
                            S_.op(act, lambda e: e.activation(sg[:], pu[:], AF.Sigmoid), reads=[pub], writes=[sgb])
                            S_.op(dve, lambda e: e.tensor_tensor(gl[:, 32 + t * 512: 32 + (t + 1) * 512], pa[:], sg[:],
                                                                 ALU.mult),
                                  reads=[pab, sgb], writes=[glb[1 + t]])

                        def stB(t):
                            pc, pcb = PS[4 + t % 2]
                            for j in range(CW):
                                mm(pc[:], dg[:, j, :], gl[:, 2 + j + t * 512: 2 + j + (t + 1) * 512],
                                   j == 0, j == CW - 1, [dgb, glb[t], glb[1 + t]], [pcb], j == CW - 1)
                            co, cob = CO.get()
                            S_.op(act, lambda e: e.activation(co[:], pc[:], AF.Identity, bias=PRA[:, cc, 8 + l: 9 + l]),
                                  reads=[pcb, PRAb], writes=[cob])
                            S_.dma(pool, CTd[cc, :, t * 512:(t + 1) * 512], co[:], cob, reads=[cob])

                        for t in range(NT + 1):
                            if t < NT:
                                stA(t)
                            if t >= 1:
                                stB(t - 1)

                for g, (wb, wbb) in pipelined(8, lambda g_: WL.load(
                        win_v[:, :, 5120 + g_ * 256: 5120 + (g_ + 1) * 256], anw)):
                    for i in range(2):
                        n = g * 2 + i
                        for t in range(NT):
                            p, pb = PS[(n * NT + t) % 4]
                            for c in range(8):
                                mm(p[:], wb[:, c, i * 128:(i + 1) * 128], actT[:, c, t * 512:(t + 1) * 512],
                                   c == 0, c == 7, [wbb] + actbufs(c, t), [pb], c == 7)
                            ro, rob = ROW.get()
                            gcol = 20 + 2 * l + n // 8
                            S_.op(act, lambda e, p=p, ro=ro, n=n, gcol=gcol: e.activation(
                                ro[:], p[:], AF.Sigmoid, bias=PRA[:, n % 8, gcol:gcol + 1]),
                                reads=[pb, PRAb], writes=[rob])
                            S_.dma(pool, Gd[n, :, t * 512:(t + 1) * 512], ro[:], rob, reads=[rob])
                S_.barrier()

            with ExitStack() as ph:
                QTt = TR(ph, "QTt", [128, S], BF16, 2, True)
                KTt = TR(ph, "KTt", [128, S], BF16, 2, True)
                VHt = TR(ph, "VHt", [128, NB, 128], BF16, 2, True)
                TBt = TR(ph, "TBt", [128, 1024], F32, 2, True)
                PT = TR(ph, "PT", [128, 1024], BF16, 6)
                STMP = TR(ph, "stmp", [128, 1024], F32, 2)
                EP = TR(ph, "ep", [128, 512], F32, 4)
                AEP = TR(ph, "aep", [128, 512], F32, 4)
                REP = TR(ph, "rep", [128, 512], F32, 4)
                LNT = TR(ph, "lnt", [128, 512], F32, 4)
                SQ3 = TR(ph, "sq3", [128, 512], BF16, 4)
                OS1 = TR(ph, "os1", [128, 512], F32, 2)
                OS2 = TR(ph, "os2", [128, 512], F32, 2)
                LS1 = TR(ph, "ls1", [128, 512], F32, 2)
                LS2 = TR(ph, "ls2", [128, 512], F32, 2)
                AOUT = TR(ph, "aout", [128, 512], BF16, 2, True) if dbg else None
                sp_rot = Rot(PSPAIR)
                O1, O1b = PS[4]
                O2, O2b = PS[5]
                L1, L1b = PS[6]
                L2, L2b = PS[7]
                DP = 3

                def load_head(h):
                    q = QTt.get(); k = KTt.get(); v = VHt.get(); tb = TBt.get()
                    S_.dma(sp, q[0][:], QTd[h], q[1], writes=[q[1]])
                    S_.dma(sp, k[0][:], KTd[h], k[1], writes=[k[1]])
                    S_.dma(sp, v[0][:], VHd[h], v[1], writes=[v[1]])
                    skew = GRd[h][127:127 + 128 * (GL_L - 1)].rearrange("(p f) -> p f", f=GL_L - 1)[:, 0:1024]
                    S_.dma(sp, tb[0][:], skew, tb[1], writes=[tb[1]])
                    return q, k, v, tb

                deferred = []
                gstep = [0]

                def flush_deferred(force=False):
                    while deferred and (force or deferred[0][0] <= gstep[0]):
                        deferred.pop(0)[1]()

                nxt = load_head(0)
                for h in range(NH):
                    (qt, qtb), (kt, ktb), (vh, vhb), (tbt, tbb) = nxt
                    if h + 1 < NH:
                        nxt = load_head(h + 1)
                    for c in range(NT):
                        nj = 4 * c + 4
                        pend = {}

                        def stA(j):
                            o = 512 * c - 128 * j
                            c0 = max(0, -o)
                            sp_, spb = sp_rot.get()
                            for hf in range(2):
                                mm(sp_[:, hf * 512 + c0:(hf + 1) * 512], kt[hf * 64:(hf + 1) * 64, j * 128:(j + 1) * 128],
                                   qt[hf * 64:(hf + 1) * 64, c * 512 + c0:(c + 1) * 512], True, True, [ktb, qtb], [spb],
                                   hf == 1)
                            pt, ptb = PT.get()
                            spv = sp_.rearrange("p (t q) -> p t q", t=2)
                            ptv = pt[:].rearrange("p (t q) -> p t q", t=2)
                            if o <= 128:
                                tm, tmb = STMP.get()
                                tmv = tm[:].rearrange("p (t q) -> p t q", t=2)
                                for hf in range(2):
                                    S_.op(dve, lambda e, hf=hf: e.tensor_tensor(
                                        tm[:, hf * 512 + c0:(hf + 1) * 512], sp_[:, hf * 512 + c0:(hf + 1) * 512],
                                        tbt[:, o + 384 + c0: o + 384 + 512], ALU.add),
                                        reads=[spb, tbb], writes=[tmb], track=(hf == 1), partial=(hf == 1))
                                S_.op(act, lambda e: e.activation(ptv[:, :, c0:512], tmv[:, :, c0:512], AF.Exp),
                                      reads=[tmb], writes=[ptb])
                            else:
                                S_.op(act, lambda e: e.activation(pt[:], sp_[:], AF.Exp, bias=B31[:, h:h + 1]),
                                      reads=[spb, B31b], writes=[ptb])
                            pend[j] = (pt, ptb, c0)
                            gstep[0] += 1

                        def stB(j):
                            pt, ptb, c0 = pend.pop(j)
                            first, last = (j == 0), (j == nj - 1)
                            for hf, (O, Ob, L, Lb) in enumerate(((O1, O1b, L1, L1b), (O2, O2b, L2, L2b))):
                                mm(O[:, c0:512], vh[:, j, :], pt[:, hf * 512 + c0:(hf + 1) * 512], first, last, [vhb, ptb],
                                   [Ob], last)
                                mm(L[:, c0:512], ones_bf[:], pt[:, hf * 512 + c0:(hf + 1) * 512], first, last,
                                   [ones_bfb, ptb], [Lb], True)

                        for j in range(nj + DP):
                            if j < nj:
                                stA(j)
                                flush_deferred()
                            if j >= DP:
                                stB(j - DP)
                        o1s, o1sb = OS1.get(); o2s, o2sb = OS2.get(); l1s, l1sb = LS1.get(); l2s, l2sb = LS2.get()
                        S_.op(dve, lambda e: e.tensor_copy(o1s[:], O1[:]), reads=[O1b], writes=[o1sb])
                        S_.op(dve, lambda e: e.tensor_copy(l1s[:], L1[:]), reads=[L1b], writes=[l1sb])
                        S_.op(dve, lambda e: e.tensor_copy(o2s[:], O2[:]), reads=[O2b], writes=[o2sb])
                        S_.op(dve, lambda e: e.tensor_copy(l2s[:], L2[:]), reads=[L2b], writes=[l2sb])

                        def part2(o1s=o1s, o1sb=o1sb, o2s=o2s, o2sb=o2sb, l1s=l1s, l1sb=l1sb, l2s=l2s, l2sb=l2sb,
                                  h=h, c=c):
                            m_, mb = EP.get(); u_, ub = EP.get(); v_, vb = EP.get(); w_, wb_ = EP.get()
                            a, ab = AEP.get()
                            S_.op(dve, lambda e: e.tensor_tensor(m_[:], l1s[:], l2s[:], ALU.mult), reads=[l1sb, l2sb], writes=[mb])
                            S_.op(dve, lambda e: e.reciprocal(m_[:], m_[:]), reads=[mb], writes=[mb])
                            S_.op(dve, lambda e: e.tensor_tensor(u_[:], o1s[:], l2s[:], ALU.mult), reads=[o1sb, l2sb], writes=[ub])
                            S_.op(dve, lambda e: e.tensor_tensor(v_[:], o2s[:], l1s[:], ALU.mult), reads=[o2sb, l1sb], writes=[vb])
                            S_.op(dve, lambda e: e.scalar_tensor_tensor(w_[:], v_[:], NLAM[:, l:l + 1], u_[:], ALU.mult, ALU.add),
                                  reads=[ub, vb, NLAMb], writes=[wb_])
                            S_.op(dve, lambda e: e.tensor_tensor(a[:], w_[:], m_[:], ALU.mult), reads=[wb_, mb], writes=[ab])
                            sq, sqb = SQ3.get()
                            S_.op(dve, lambda e: e.tensor_tensor(sq[:], a[:], a[:], ALU.mult), reads=[ab], writes=[sqb])
                            deferred.append([gstep[0] + 8, lambda: part2s(a, ab, sq, sqb, h, c)])

                        def part2s(a, ab, sq, sqb, h, c):
                            ssp, sspb = sp_rot.get()
                            R, Rb = REP.get()
                            lt, ltb = LNT.get()
                            mm(ssp[:, 0:512], ones_bf[:], sq[:], True, True, [ones_bfb, sqb], [sspb], True)
                            S_.op(dve, lambda e: e.tensor_copy(lt[:], ssp[:, 0:512]), reads=[sspb], writes=[ltb])
                            deferred.append([gstep[0] + 3, lambda: part2b(a, ab, R, Rb, lt, ltb, h, c)])

                        def part2b(a, ab, R, Rb, lt, ltb, h, c):
                            S_.op(act, lambda e: e.activation(lt[:], lt[:], AF.Ln, bias=EPS, scale=1.0 / 128),
                                  reads=[ltb], writes=[ltb])
                            S_.op(act, lambda e: e.activation(R[:], lt[:], AF.Exp, scale=-0.5), reads=[ltb], writes=[Rb])
                            S_.op(dve, lambda e: e.scalar_tensor_tensor(
                                actT[:, h, c * 512:(c + 1) * 512], a[:], SL[:, l:l + 1], R[:], ALU.mult, ALU.mult),
                                reads=[ab, Rb, SLb], writes=actbufs(h, c))
                            if dbg:
                                ao, aob = AOUT.get()
                                S_.op(dve, lambda e: e.tensor_copy(ao[:], actT[:, h, c * 512:(c + 1) * 512]),
                                      reads=actbufs(h, c), writes=[aob])
                                S_.dma(pool, ATd[h, :, c * 512:(c + 1) * 512], ao[:], aob, reads=[aob])

                        deferred.append([gstep[0] + 1, part2])
                flush_deferred(True)
                S_.barrier()

            with ExitStack() as ph:
                W4S = TR(ph, "w4_s", [128, 8, 256], F32, 1, True)
                WPA, WPAb = T(ph, "WPA", [128, 8, 1024], BF16)
                WPC, WPCb = T(ph, "WPC", [128, 8, 1024], BF16)
                WOU, WOUb = T(ph, "WOU", [128, 8, 1024], BF16)
                for (wt, wtb, src) in ((WPA, WPAb, w_proj_attn), (WPC, WPCb, w_proj_conv), (WOU, WOUb, w_out)):
                    sv = src[l].rearrange("(c p) n -> p c n", p=128)
                    for g in range(4):
                        sg, sgb = W4S.get()
                        S_.dma(sp, sg[:], sv[:, :, g * 256:(g + 1) * 256], sgb, writes=[sgb])
                        S_.op(dve, lambda e, wt=wt, sg=sg, g=g: e.tensor_copy(wt[:, 0:4, g * 256:(g + 1) * 256], sg[:, 0:4, :]),
                              reads=[sgb], writes=[wtb], partial=True)
                        S_.op(act, lambda e, wt=wt, sg=sg, g=g: e.activation(wt[:, 4:8, g * 256:(g + 1) * 256], sg[:, 4:8, :], AF.Copy),
                              reads=[sgb], writes=[wtb], partial=True)
                X4 = TR(ph, "X4", [128, 8, 256], F32, 2, True)
                C4 = TR(ph, "C4", [128, 8, 256], F32, 2, True)
                G4 = TR(ph, "G4", [128, 16, 256], BF16, 2, True)
                C4b = TR(ph, "C4b", [128, 8, 256], BF16, 1)
                SQ4 = TR(ph, "SQ4", [128, 8, 256], BF16, 1)
                M4 = TR(ph, "M4", [128, 8, 256], BF16, 1)
                E4 = TR(ph, "E4", [128, 256], F32, 8)
                gd_v = Gd.rearrange("n p s -> p n s")
                ct_v = CTd.rearrange("c p s -> p c s")
                x1_v = X1d.rearrange("(c p) s -> p c s", p=128)

                def loads4(s):
                    x = X4.get(); cx = C4.get(); g4 = G4.get()
                    sl = slice(s * 256, (s + 1) * 256)
                    S_.dma(sp, x[0][:], xin_v[:, :, sl], x[1], writes=[x[1]])
                    S_.dma(sp, cx[0][:], ct_v[:, :, sl], cx[1], writes=[cx[1]])
                    S_.dma(sp, g4[0][:], gd_v[:, :, sl], g4[1], writes=[g4[1]])
                    return x, cx, g4

                CN2 = TR(ph, "CN2_", [128, 8, 256], BF16, 2)
                ps1, ps1b = PS[0]
                ps2, ps2b = PS[1]

                def stageA(s, c4, c4b):
                    cb16, cb16b = C4b.get(); sq, sqb = SQ4.get(); cn, cnb = CN2.get()
                    d4, d4b = c4, c4b
                    S_.op(act, lambda e: e.activation(sq[:], c4[:], AF.Square), reads=[c4b], writes=[sqb])
                    S_.op(dve, lambda e: e.tensor_copy(cb16[:], c4[:]), reads=[c4b], writes=[cb16b])
                    for c in range(8):
                        mm(ps1[:, 0:256], ones_bf[:], cb16[:, c, :], c == 0, c == 7, [ones_bfb, cb16b], [ps1b], c == 7)
                    for c in range(8):
                        mm(ps2[:, 0:256], ones_bf[:], sq[:, c, :], c == 0, c == 7, [ones_bfb, sqb], [ps2b], c == 7)
                    mu, mub = E4.get(); msq, msqb = E4.get(); var, varb = E4.get()
                    S_.op(dve, lambda e: e.tensor_scalar(mu[:], ps1[:, 0:256], 1.0 / D, None, ALU.mult),
                          reads=[ps1b], writes=[mub])
                    S_.op(dve, lambda e: e.tensor_tensor(msq[:], mu[:], mu[:], ALU.mult), reads=[mub], writes=[msqb])
                    S_.op(dve, lambda e: e.scalar_tensor_tensor(var[:], ps2[:, 0:256], 1.0 / D, msq[:], ALU.mult,
                                                                ALU.subtract),
                          reads=[ps2b, msqb], writes=[varb])
                    S_.op(act, lambda e: e.activation(var[:], var[:], AF.Ln, bias=EPS, scale=1.0),
                          reads=[varb], writes=[varb])
                    S_.op(act, lambda e: e.activation(var[:], var[:], AF.Exp, scale=-0.5), reads=[varb], writes=[varb])
                    for c in range(8):
                        S_.op(dve, lambda e, c=c: e.tensor_tensor(d4[:, c, :], c4[:, c, :], mu[:], ALU.subtract),
                              reads=[c4b, mub], writes=[d4b], partial=True)
                        S_.op(dve, lambda e, c=c: e.tensor_tensor(d4[:, c, :], d4[:, c, :], var[:], ALU.mult),
                              reads=[d4b, varb], writes=[d4b], partial=True)
                        S_.op(act, lambda e, c=c: e.activation(
                            cn[:, c, :], d4[:, c, :], AF.Silu, bias=PRA[:, c, 16 + l:17 + l],
                            scale=PRA[:, c, 12 + l:13 + l]),
                            reads=[d4b, PRAb], writes=[cnb], partial=True)
                    return cn, cnb

                def stageB1(s, cn, cnb, g4, g4b):
                    sl = slice(s * 256, (s + 1) * 256)
                    m4, m4b = M4.get()
                    for n in range(8):
                        pa, pab = PS[2 + (n % 2) * 2]
                        pbb_, pbbb = PS[3 + (n % 2) * 2]
                        for c in range(8):
                            mm(pa[:, 0:256], WPA[:, c, n * 128:(n + 1) * 128], actT[:, c, sl], c == 0, c == 7,
                               [WPAb, ACTB[c][s]], [pab], c == 7)
                        for c in range(8):
                            mm(pbb_[:, 0:256], WPC[:, c, n * 128:(n + 1) * 128], cn[:, c, :], c == 0, c == 7,
                               [WPCb, cnb], [pbbb], c == 7)
                        ta, tab_ = E4.get(); tb, tbb_ = E4.get()
                        S_.op(dve, lambda e, n=n, ta=ta, pa=pa: e.tensor_tensor(ta[:], pa[:, 0:256], g4[:, n, :], ALU.mult),
                              reads=[pab, g4b], writes=[tab_])
                        S_.op(dve, lambda e, n=n, tb=tb, pbb_=pbb_: e.tensor_tensor(tb[:], pbb_[:, 0:256], g4[:, 8 + n, :],
                                                                                   ALU.mult),
                              reads=[pbbb, g4b], writes=[tbb_])
                        S_.op(dve, lambda e, n=n, ta=ta, tb=tb: e.tensor_tensor(m4[:, n, :], ta[:], tb[:], ALU.add),
                              reads=[tab_, tbb_], writes=[m4b], partial=True)
                    return m4, m4b

                def stageB2(s, m4, m4b, x4, x4b):
                    sl = slice(s * 256, (s + 1) * 256)
                    for n in range(8):
                        po, pob = PS[6 + n % 2]
                        for c in range(8):
                            mm(po[:, 0:256], WOU[:, c, n * 128:(n + 1) * 128], m4[:, c, :], c == 0, c == 7,
                               [WOUb, m4b], [pob], c == 7)
                        S_.op(dve, lambda e, n=n, po=po: e.tensor_tensor(x4[:, n, :], x4[:, n, :], po[:, 0:256], ALU.add),
                              reads=[pob, x4b], writes=[x4b], partial=True)
                    S_.dma(pool, x1_v[:, :, sl], x4[:], x4b, reads=[x4b])
                    sq2, sq2b = SQ4.get()
                    S_.op(act, lambda e: e.activation(sq2[:], x4[:], AF.Square), reads=[x4b], writes=[sq2b])
                    for c in range(8):
                        mm(ps1[:, 256:512], ones_bf[:], sq2[:, c, :], c == 0, c == 7, [ones_bfb, sq2b], [ps1b], c == 7)
                    R, Rb = E4.get()
                    S_.op(act, lambda e: e.activation(R[:], ps1[:, 256:512], AF.Ln, bias=EPS, scale=1.0 / D),
                          reads=[ps1b], writes=[Rb])
                    S_.op(act, lambda e: e.activation(R[:], R[:], AF.Exp, scale=-0.5), reads=[Rb], writes=[Rb])
                    for c in range(8):
                        S_.op(dve, lambda e, c=c: e.tensor_tensor(actT[:, c, sl], x4[:, c, :], R[:], ALU.mult),
                              reads=[x4b, Rb], writes=[ACTB[c][s]])

                cur = loads4(0)
                cn_cur = stageA(0, cur[1][0], cur[1][1])
                for s in range(NS):
                    (x4, x4b), _, (g4, g4b) = cur
                    nxt4 = loads4(s + 1) if s + 1 < NS else None
                    m4, m4b = stageB1(s, cn_cur[0], cn_cur[1], g4, g4b)
                    if nxt4 is not None:
                        cn_nxt = stageA(s + 1, nxt4[1][0], nxt4[1][1])
                    stageB2(s, m4, m4b, x4, x4b)
                    if nxt4 is not None:
                        cur, cn_cur = nxt4, cn_nxt
                S_.barrier()

            with ExitStack() as ph:
                WG = WLoader(ph, "wg", 8, 128, 2, 2)
                WU = WLoader(ph, "wu", 8, 128, 2, 2)
                WD = WLoader(ph, "wd", NF, 128, 1, 2)
                HID, _ = T(ph, "HID", [128, NF, FT], BF16)
                HIDB = [[S_.buf(f"hid{f}_{tt}") for tt in range(TCS)] for f in range(NF)]
                SGF = TR(ph, "sgf", [128, 512], F32, 3)
                XF = TR(ph, "xf", [128, 512], F32, 4, True)
                fnw = (PRA[:, :, 4 + l], PRAb)
                wgu_v = w_gate_up[l].rearrange("(c p) n -> p c n", p=128)
                wd_v = w_down[l].rearrange("(f p) n -> p f n", p=128)
                for p_ in range(NP):
                    for f, ((wg, wgb), (wu, wub)) in pipelined(NF, lambda f_: (
                            WG.load(wgu_v[:, :, f_ * 128:(f_ + 1) * 128], fnw),
                            WU.load(wgu_v[:, :, FF + f_ * 128: FF + (f_ + 1) * 128], fnw))):
                        for tt in range(TCS):
                            t = p_ * TCS + tt
                            k = f * TCS + tt
                            pg, pgb = PS[(2 * k) % 4]
                            pu, pub = PS[(2 * k + 1) % 4]
                            for c in range(8):
                                mm(pg[:], wg[:, c, :], actT[:, c, t * 512:(t + 1) * 512], c == 0, c == 7,
                                   [wgb] + actbufs(c, t), [pgb], c == 7)
                            for c in range(8):
                                mm(pu[:], wu[:, c, :], actT[:, c, t * 512:(t + 1) * 512], c == 0, c == 7,
                                   [wub] + actbufs(c, t), [pub], c == 7)
                            sg, sgb = SGF.get()
                            S_.op(act, lambda e, sg=sg, pg=pg: e.activation(sg[:], pg[:], AF.Silu), reads=[pgb], writes=[sgb])
                            S_.op(dve, lambda e, sg=sg, pu=pu, f=f, tt=tt: e.tensor_tensor(
                                HID[:, f, tt * 512:(tt + 1) * 512], sg[:], pu[:], ALU.mult),
                                reads=[sgb, pub], writes=[HIDB[f][tt]])
                    for n, (wd, wdb) in pipelined(8, lambda n_: WD.load(wd_v[:, :, n_ * 128:(n_ + 1) * 128])):
                        for tt in range(TCS):
                            t = p_ * TCS + tt
                            xf, xfb = XF.get()
                            S_.dma(sp, xf[:], X1d[n * 128:(n + 1) * 128, t * 512:(t + 1) * 512], xfb, writes=[xfb])
                            pd, pdb = PS[4 + (n * TCS + tt) % 4]
                            for f in range(NF):
                                mm(pd[:], wd[:, f, :], HID[:, f, tt * 512:(tt + 1) * 512], f == 0, f == NF - 1,
                                   [wdb, HIDB[f][tt]], [pdb], f == NF - 1)
                            S_.op(dve, lambda e, xf=xf, pd=pd: e.tensor_tensor(xf[:], xf[:], pd[:], ALU.add),
                                  reads=[xfb, pdb], writes=[xfb])
                            S_.dma(pool, xout[n * 128:(n + 1) * 128, t * 512:(t + 1) * 512], xf[:], xfb, reads=[xfb])
                S_.barrier()
        S_.emit()
    return nc


_NC_CACHE = {}


def kernel(x, attn_norm_w, w_in, q_norm_w, k_norm_w, lam_q1, lam_k1, lam_q2, lam_k2,
           subln_w, w_proj_attn, conv_w, conv_b, conv_ln_w, conv_ln_b, w_proj_conv,
           gate_b, w_out, ffn_norm_w, w_gate_up, w_down, rel_bias):
    x = np.asarray(x)
    B, S, _ = x.shape
    if "nc" not in _NC_CACHE:
        _NC_CACHE["nc"] = build(S, 4)
    nc = _NC_CACHE["nc"]
    f = lambda a: np.ascontiguousarray(np.asarray(a, dtype=np.float32))
    shared = dict(attn_norm_w=f(attn_norm_w), w_in=f(w_in), q_norm_w=f(q_norm_w), k_norm_w=f(k_norm_w),
                  lam_q1=f(lam_q1), lam_k1=f(lam_k1), lam_q2=f(lam_q2), lam_k2=f(lam_k2), subln_w=f(subln_w),
                  w_proj_attn=f(w_proj_attn), conv_w=f(conv_w), conv_b=f(conv_b), conv_ln_w=f(conv_ln_w),
                  conv_ln_b=f(conv_ln_b), w_proj_conv=f(w_proj_conv), gate_b=f(gate_b), w_out=f(w_out),
                  ffn_norm_w=f(ffn_norm_w), w_gate_up=f(w_gate_up), w_down=f(w_down), rel_bias=f(rel_bias),
                  oh=onehot_table())
    in_maps = []
    for b in range(B):
        m = dict(shared)
        m["xT"] = np.ascontiguousarray(x[b].T)
        in_maps.append(m)
    res = run_bass_kernel_spmd(nc, in_maps, core_ids=list(range(B)))
    out = np.stack([np.ascontiguousarray(res.results[b]["outT"].T) for b in range(B)], axis=0)
    return out.astype(np.float32)
```
